# Optimizing a Trainium2 kernel written in Bass

```python
import jax, jax.numpy as jnp
from jax import lax
import numpy as np

D_MODEL = 1024
BATCH = 4
SEQ = 4096
DEPTH = 2
DEC_BATCH = 128
DEC_SEQ = 8
PAST_LEN = 16384
PAGE_SIZE = 128

N_MIXERS = 2
N_ATTN_LAYERS = (DEPTH + 1) // 2
N_RET_LAYERS = DEPTH // 2
PLE_DIM = 256
ROPE_THETA = 10000.0
NORM_EPS = 1e-6
NEG_INF = -1e30
WINDOW = 128
ATTN_HEADS = 16
ATTN_KV_HEADS = 4
ATTN_HEAD_DIM = 64
ATTN_GROUP = ATTN_HEADS // ATTN_KV_HEADS
ATTN_Q_DIM = ATTN_HEADS * ATTN_HEAD_DIM
ATTN_KV_DIM = ATTN_KV_HEADS * ATTN_HEAD_DIM
ATTN_IN_DIM = 2 * ATTN_Q_DIM + 2 * ATTN_KV_DIM
RET_HEADS = 4
RET_QK_DIM = D_MODEL // RET_HEADS
RET_V_DIM = 2 * RET_QK_DIM
RET_VW = RET_HEADS * RET_V_DIM
RET_IN_DIM = 2 * D_MODEL + 2 * RET_VW
RET_CHUNK = 128

kernel_name = 'hybrid_swa_sink_retention_step'


def rms_norm(x, g):
    xf = x.astype(jnp.float32)
    y = xf * lax.rsqrt(jnp.mean(xf * xf, axis=-1, keepdims=True) + NORM_EPS)
    return (y * g.astype(jnp.float32)).astype(x.dtype)


def rope(x, pos):
    half = x.shape[-1] // 2
    inv = ROPE_THETA ** (-jnp.arange(half, dtype=jnp.float32) / half)
    ang = pos.astype(jnp.float32)[:, None] * inv[None, :]
    cos = jnp.cos(ang)[None, :, None, :]
    sin = jnp.sin(ang)[None, :, None, :]
    xf = x.astype(jnp.float32)
    x1, x2 = xf[..., :half], xf[..., half:]
    return jnp.concatenate([x1 * cos - x2 * sin, x2 * cos + x1 * sin], axis=-1).astype(x.dtype)


def sink_softmax(s, sink, mask):
    s = jnp.where(mask, s, NEG_INF)
    m = jnp.maximum(jnp.max(s, axis=-1, keepdims=True), sink)
    p = jnp.exp(s - m)
    return p / (jnp.sum(p, axis=-1, keepdims=True) + jnp.exp(sink - m))


def swa_project(h, w_in, pos):
    B, T, _ = h.shape
    z = h @ w_in
    q, k, v, g = jnp.split(z, [ATTN_Q_DIM, ATTN_Q_DIM + ATTN_KV_DIM, ATTN_Q_DIM + 2 * ATTN_KV_DIM], axis=-1)
    q = rope(q.reshape(B, T, ATTN_HEADS, ATTN_HEAD_DIM), pos)
    k = rope(k.reshape(B, T, ATTN_KV_HEADS, ATTN_HEAD_DIM), pos)
    v = v.reshape(B, T, ATTN_KV_HEADS, ATTN_HEAD_DIM)
    return q, k, v, g


def swa_prompt(h, w_in, sinks, w_out):
    B, T, _ = h.shape
    nb = T // WINDOW
    q, k, v, g = swa_project(h, w_in, jnp.arange(T))
    qb = q.reshape(B, nb, WINDOW, ATTN_KV_HEADS, ATTN_GROUP, ATTN_HEAD_DIM)
    kb = k.reshape(B, nb, WINDOW, ATTN_KV_HEADS, ATTN_HEAD_DIM)
    vb = v.reshape(B, nb, WINDOW, ATTN_KV_HEADS, ATTN_HEAD_DIM)
    kp = jnp.concatenate([jnp.zeros_like(kb[:, :1]), kb], axis=1)
    vp = jnp.concatenate([jnp.zeros_like(vb[:, :1]), vb], axis=1)
    k_band = jnp.concatenate([kp[:, :-1], kp[:, 1:]], axis=2)
    v_band = jnp.concatenate([vp[:, :-1], vp[:, 1:]], axis=2)
    s = jnp.einsum('bnqkgd,bnskd->bnkgqs', qb, k_band, preferred_element_type=jnp.float32) * (ATTN_HEAD_DIM ** -0.5)
    qi = jnp.arange(WINDOW)[:, None] + WINDOW
    kj = jnp.arange(2 * WINDOW)[None, :]
    diff = qi - kj
    band = (diff >= 0) & (diff < WINDOW)
    not_before_start = (jnp.arange(nb)[:, None, None] > 0) | (kj[None] >= WINDOW)
    mask = (band[None] & not_before_start)[None, :, None, None]
    sink = sinks.astype(jnp.float32).reshape(ATTN_KV_HEADS, ATTN_GROUP, 1, 1)
    p = sink_softmax(s, sink, mask).astype(v.dtype)
    o = jnp.einsum('bnkgqs,bnskd->bnqkgd', p, v_band).reshape(B, T, ATTN_Q_DIM)
    y = (o * jax.nn.silu(g)) @ w_out
    return y, k[:, T - WINDOW:], v[:, T - WINDOW:]


def swa_sample(h, k_buf, v_buf, w_in, sinks, w_out):
    B, L, _ = h.shape
    pos = PAST_LEN + jnp.arange(L)
    q, k, v, g = swa_project(h, w_in, pos)
    kc = jnp.concatenate([k_buf, k], axis=1)
    vc = jnp.concatenate([v_buf, v], axis=1)
    kpos = jnp.concatenate([PAST_LEN - WINDOW + jnp.arange(WINDOW), pos])
    diff = pos[:, None] - kpos[None, :]
    mask = ((diff >= 0) & (diff < WINDOW))[None, None, None]
    qg = q.reshape(B, L, ATTN_KV_HEADS, ATTN_GROUP, ATTN_HEAD_DIM)
    s = jnp.einsum('bqkgd,bskd->bkgqs', qg, kc, preferred_element_type=jnp.float32) * (ATTN_HEAD_DIM ** -0.5)
    sink = sinks.astype(jnp.float32).reshape(ATTN_KV_HEADS, ATTN_GROUP, 1, 1)
    p = sink_softmax(s, sink, mask).astype(vc.dtype)
    o = jnp.einsum('bkgqs,bskd->bqkgd', p, vc).reshape(B, L, ATTN_Q_DIM)
    y = (o * jax.nn.silu(g)) @ w_out
    return y, kc[:, -WINDOW:], vc[:, -WINDOW:]


def ret_log_gamma():
    return jnp.log1p(-(2.0 ** (-5.0 - jnp.arange(RET_HEADS, dtype=jnp.float32))))


def ret_project(h, w_in, pos):
    B, T, _ = h.shape
    z = h @ w_in
    q, k, v, g = jnp.split(z, [D_MODEL, 2 * D_MODEL, 2 * D_MODEL + RET_VW], axis=-1)
    q = rope(q.reshape(B, T, RET_HEADS, RET_QK_DIM), pos)
    k = rope(k.reshape(B, T, RET_HEADS, RET_QK_DIM), pos) * (RET_QK_DIM ** -0.5)
    v = v.reshape(B, T, RET_HEADS, RET_V_DIM)
    return q, k, v, g


def retention_chunk(state, q, k, v, lg):
    L = q.shape[1]
    q = q.astype(jnp.float32)
    k = k.astype(jnp.float32)
    v = v.astype(jnp.float32)
    idx = jnp.arange(L, dtype=jnp.float32)
    diff = idx[:, None] - idx[None, :]
    dmask = jnp.where(diff[None] >= 0, jnp.exp(jnp.maximum(diff, 0.0)[None] * lg[:, None, None]), 0.0)
    a = jnp.einsum('bqhd,bshd->bhqs', q, k) * dmask[None]
    inner = jnp.einsum('bhqs,bshe->bqhe', a, v)
    cross = jnp.einsum('bqhd,bhde->bqhe', q, state) * jnp.exp((idx + 1.0)[:, None] * lg[None])[None, :, :, None]
    k_dec = k * jnp.exp((L - 1.0 - idx)[:, None] * lg[None])[None, :, :, None]
    new_state = jnp.exp(L * lg)[None, :, None, None] * state + jnp.einsum('bshd,bshe->bhde', k_dec, v)
    return new_state, inner + cross


def ret_output(o, g, w_out, dtype):
    B, T = o.shape[0], o.shape[1]
    mu = jnp.mean(o, axis=-1, keepdims=True)
    var = jnp.mean(jnp.square(o - mu), axis=-1, keepdims=True)
    on = ((o - mu) * lax.rsqrt(var + NORM_EPS)).reshape(B, T, RET_VW).astype(dtype)
    return (on * jax.nn.silu(g)) @ w_out


def ret_prompt(h, w_in, w_out):
    B, T, _ = h.shape
    nc = T // RET_CHUNK
    q, k, v, g = ret_project(h, w_in, jnp.arange(T))
    lg = ret_log_gamma()
    to_chunks = lambda a: jnp.moveaxis(a.reshape(B, nc, RET_CHUNK, a.shape[2], a.shape[3]), 1, 0)

    def step(state, qkv):
        qc, kc, vc = qkv
        return retention_chunk(state, qc, kc, vc, lg)

    state0 = jnp.zeros((B, RET_HEADS, RET_QK_DIM, RET_V_DIM), jnp.float32)
    state, o = lax.scan(step, state0, (to_chunks(q), to_chunks(k), to_chunks(v)))
    o = jnp.moveaxis(o, 0, 1).reshape(B, T, RET_HEADS, RET_V_DIM)
    return ret_output(o, g, w_out, h.dtype), state


def ret_sample(h, state, w_in, w_out):
    L = h.shape[1]
    q, k, v, g = ret_project(h, w_in, PAST_LEN + jnp.arange(L))
    new_state, o = retention_chunk(state.astype(jnp.float32), q, k, v, ret_log_gamma())
    return ret_output(o, g, w_out, h.dtype), new_state.astype(state.dtype)


def residual_update(x, y, p, g_post, w_ple, w_gate):
    x = x + rms_norm(y, g_post)
    return x + jax.nn.sigmoid(x @ w_gate) * (p @ w_ple)


def setup_inputs(seed: int = 0) -> dict:
    key = jax.random.key(seed)
    ks = jax.random.split(key, 18)
    f32 = jnp.float32
    nrm = lambda k, shape, scale: jax.random.normal(k, shape, f32) * scale
    return {
        'x_prompt': nrm(ks[0], (BATCH, SEQ, D_MODEL), 1.0),
        'x_sample': nrm(ks[1], (DEC_BATCH, DEC_SEQ, D_MODEL), 1.0),
        'cache_k_win': nrm(ks[2], (N_ATTN_LAYERS, DEC_BATCH, WINDOW, ATTN_KV_HEADS, ATTN_HEAD_DIM), 1.0),
        'cache_v_win': nrm(ks[3], (N_ATTN_LAYERS, DEC_BATCH, WINDOW, ATTN_KV_HEADS, ATTN_HEAD_DIM), 1.0),
        'state_ret': nrm(ks[4], (N_RET_LAYERS, DEC_BATCH, RET_HEADS, RET_QK_DIM, RET_V_DIM), 0.1),
        'p_prompt': nrm(ks[5], (DEPTH, BATCH, SEQ, PLE_DIM), 1.0),
        'p_sample': nrm(ks[6], (DEPTH, DEC_BATCH, DEC_SEQ, PLE_DIM), 1.0),
        'pre_norm': 1.0 + nrm(ks[7], (DEPTH, D_MODEL), 0.1),
        'post_norm': 1.0 + nrm(ks[8], (DEPTH, D_MODEL), 0.1),
        'w_in_attn': nrm(ks[9], (N_ATTN_LAYERS, D_MODEL, ATTN_IN_DIM), D_MODEL ** -0.5),
        'attn_sinks': nrm(ks[10], (N_ATTN_LAYERS, ATTN_HEADS), 0.5),
        'w_out_attn': nrm(ks[11], (N_ATTN_LAYERS, ATTN_Q_DIM, D_MODEL), ATTN_Q_DIM ** -0.5),
        'w_in_ret': nrm(ks[12], (N_RET_LAYERS, D_MODEL, RET_IN_DIM), D_MODEL ** -0.5),
        'w_out_ret': nrm(ks[13], (N_RET_LAYERS, RET_VW, D_MODEL), RET_VW ** -0.5),
        'w_ple': nrm(ks[14], (DEPTH, PLE_DIM, D_MODEL), PLE_DIM ** -0.5),
        'w_ple_gate': nrm(ks[15], (DEPTH, D_MODEL, D_MODEL), D_MODEL ** -0.5),
    }


def reference(x_prompt, x_sample, cache_k_win, cache_v_win, state_ret, p_prompt, p_sample,
              pre_norm, post_norm, w_in_attn, attn_sinks, w_out_attn, w_in_ret, w_out_ret,
              w_ple, w_ple_gate):
    xp, xs = x_prompt, x_sample
    kwp, vwp, kws, vws, srp, srs = [], [], [], [], [], []
    for i in range(DEPTH):
        j = i // N_MIXERS
        hp = rms_norm(xp, pre_norm[i])
        hs = rms_norm(xs, pre_norm[i])
        if i % N_MIXERS == 0:
            yp, kp, vp = swa_prompt(hp, w_in_attn[j], attn_sinks[j], w_out_attn[j])
            ys, kn, vn = swa_sample(hs, cache_k_win[j], cache_v_win[j], w_in_attn[j], attn_sinks[j], w_out_attn[j])
            kwp.append(kp); vwp.append(vp); kws.append(kn); vws.append(vn)
        else:
            yp, sp = ret_prompt(hp, w_in_ret[j], w_out_ret[j])
            ys, sn = ret_sample(hs, state_ret[j], w_in_ret[j], w_out_ret[j])
            srp.append(sp.astype(state_ret.dtype)); srs.append(sn)
        xp = residual_update(xp, yp, p_prompt[i], post_norm[i], w_ple[i], w_ple_gate[i])
        xs = residual_update(xs, ys, p_sample[i], post_norm[i], w_ple[i], w_ple_gate[i])
    k_win_prompt = jnp.stack(kwp)
    v_win_prompt = jnp.stack(vwp)
    k_win_sample = jnp.stack(kws)
    v_win_sample = jnp.stack(vws)
    ret_state_prompt = jnp.stack(srp)
    ret_state_sample = jnp.stack(srs)
    return (xp, xs, k_win_prompt, v_win_prompt, k_win_sample, v_win_sample, ret_state_prompt, ret_state_sample)
```

```python
import numpy as np
import concourse.bass as bass
import concourse.mybir as mybir
from concourse.alu_op_type import AluOpType as ALU
from concourse.bass_utils import run_bass_kernel_spmd

F32 = mybir.dt.float32
BF16 = mybir.dt.bfloat16
AF = mybir.ActivationFunctionType

D = 1024
HD = 64
NH = 16
NKV = 4
PLE = 256
RH = 4
DK = 256
DV = 512
RVW = 2048
EPS = 1e-6
NEG = -30000.0
PAST = 16384
THETA = 10000.0
NBS = 16
LS = 8


class Buf:
    __slots__ = ("name", "w", "r", "dsem", "dcnt", "last_dma")

    ALL = []

    def __init__(self, name):
        self.name = name
        self.w = None
        self.r = []
        self.dsem = None
        self.dcnt = 0
        Buf.ALL.append(self)


class Rec:
    __slots__ = ("kind", "eng", "fns", "reads", "writes", "preds", "cost", "ev", "sbuf", "out", "in_", "kw",
                 "is_out", "fin", "ctx")


class Sched:
    def __init__(self, nc):
        self.nc = nc
        self.E = {"pe": nc.tensor, "act": nc.scalar, "dve": nc.vector, "pool": nc.gpsimd, "sp": nc.sync}
        self.sem = {e: nc.alloc_semaphore("cnt_" + e) for e in self.E}
        self.cnt = {e: 0 for e in self.E}
        self.waited = {e: {} for e in self.E}
        self.semobj = {}
        self.out_events = []
        self.nsem = 0
        self.dbufs = []
        self.recs = []
        self.reorder = True
        self.ctx = None

    def _record(self, r, reads, writes, waw=True):
        idx = len(self.recs)
        preds = set()
        for b in reads:
            if b.w is not None:
                preds.add(b.w)
        import os
        nw = os.environ.get("KSCHED_NOWAR", "")
        for b in writes:
            nowar = bool(nw) and (nw == "1" or any(b.name.startswith(p) for p in nw.split(",")))
            if waw and b.w is not None and not nowar:
                preds.add(b.w)
            if waw and not nowar:
                preds.update(b.r)
        r.preds = preds
        r.ev = None
        r.ctx = self.ctx
        self.recs.append(r)
        for b in reads:
            b.r.append(idx)
        for b in writes:
            b.w = idx
            b.r = []

    def op(self, e, fn, reads=(), writes=(), n=None):
        r = Rec()
        r.kind, r.eng, r.fns = "op", e, [fn]
        if e == "pool":
            r.cost = 0.5 + (n or 256) / 450.0
        elif e == "act":
            r.cost = 0.3 + (n or 512) / 800.0
        else:
            r.cost = 0.25 + (n or 256) / 1000.0
        self._record(r, reads, writes)

    def pe(self, fns, reads=(), writes=(), ncols=128):
        r = Rec()
        r.kind, r.eng, r.fns = "pe", "pe", list(fns)
        r.cost = 0.1 + len(fns) * (0.04 + ncols / 1950.0)
        self._record(r, reads, writes)

    def dma(self, q, out, in_, reads=(), writes=(), sbuf=None, is_out=False, waw=True, **kw):
        r = Rec()
        r.kind, r.eng, r.fns = "dma", q, None
        r.out, r.in_, r.kw, r.sbuf, r.is_out = out, in_, kw, sbuf, is_out
        r.cost = 2.5
        self._record(r, reads, writes, waw)
        prev = getattr(sbuf, "last_dma", None)
        if prev is not None and prev[0] is self.recs:
            r.preds.add(prev[1])
        sbuf.last_dma = (self.recs, len(self.recs) - 1)

    def _order(self):
        import heapq
        recs = self.recs
        n = len(recs)
        if not self.reorder:
            return list(range(n))
        succ = [[] for _ in range(n)]
        npred = [0] * n
        for i, r in enumerate(recs):
            npred[i] = len(r.preds)
            for p in r.preds:
                succ[p].append(i)
        ready_t = [0.0] * n
        efree = {e: 0.0 for e in self.E}
        heap = [(0.0, i) for i in range(n) if npred[i] == 0]
        heapq.heapify(heap)
        order = []
        import os
        dbg = bool(os.environ.get("KSCHED_DEBUG"))
        crit, engprev, st_, last_on = {}, {}, {}, {}
        while heap:
            t, i = heapq.heappop(heap)
            r = recs[i]
            if r.kind == "dma":
                st = max(t, efree[r.eng])
                efree[r.eng] = st + 0.1
                fin = st + r.cost
            else:
                st = max(t, efree[r.eng])
                fin = st + r.cost
                efree[r.eng] = fin
            r.fin = fin
            if dbg:
                st_[i] = st
                engprev[i] = last_on.get(r.eng, -1) if st > t + 1e-9 else -1
                last_on[r.eng] = i
            order.append(i)
            for s_ in succ[i]:
                lat = 0.15 if recs[s_].eng != r.eng else 0.05
                if fin + lat > ready_t[s_]:
                    ready_t[s_] = fin + lat
                    if dbg:
                        crit[s_] = i
                npred[s_] -= 1
                if npred[s_] == 0:
                    heapq.heappush(heap, (ready_t[s_], s_))
        assert len(order) == n
        self.est_us = max(r.fin for r in recs) if recs else 0.0
        import os
        if os.environ.get("KSCHED_DEBUG"):
            busy = {}
            for r in recs:
                busy[r.eng] = busy.get(r.eng, 0.0) + (r.cost if r.kind != 'dma' else 0.1)
            print("SCHED n=%d est=%.1fus busy=%s" % (n, self.est_us, {k: round(v) for k, v in busy.items()}))
            i = max(range(n), key=lambda k: recs[k].fin)
            on_eng, via_dep, via_eng = {}, 0, 0
            while i >= 0:
                r = recs[i]
                on_eng[r.eng] = on_eng.get(r.eng, 0.0) + r.cost
                if engprev.get(i, -1) >= 0:
                    via_eng += 1
                    i = engprev[i]
                elif i in crit:
                    via_dep += 1
                    i = crit[i]
                else:
                    break
            print("   critical chain: time on engines", {k: round(v) for k, v in on_eng.items()}, "dep-hops", via_dep, "engine-queue-hops", via_eng)
        return order

    def _waits(self, e, preds):
        need = {}
        for p in preds:
            sem, val = self.recs[p].ev
            k = id(sem)
            self.semobj[k] = sem
            if need.get(k, 0) < val:
                need[k] = val
        out = []
        wd = self.waited[e]
        for k, v in need.items():
            if wd.get(k, 0) >= v:
                continue
            wd[k] = v
            out.append((self.semobj[k], v))
        return out

    def flush(self):
        for i in self._order():
            r = self.recs[i]
            e = r.eng
            eng = self.E[e]
            if r.ctx is not None:
                r.ctx()
            deps = self._waits(e, r.preds)
            if r.kind == "dma":
                for sem, v in deps:
                    eng.wait_ge(sem, v)
                sb_ = r.sbuf
                if sb_.dsem is None:
                    sb_.dsem = self.nc.alloc_semaphore("d_%s_%d" % (sb_.name, self.nsem))
                    self.nsem += 1
                    self.dbufs.append(sb_)
                sb_.dcnt += 1
                eng.dma_start(out=r.out, in_=r.in_, **r.kw).then_inc(sb_.dsem, 16)
                r.ev = (sb_.dsem, 16 * sb_.dcnt)
                if r.is_out:
                    self.out_events.append(r.ev)
                continue
            if r.kind == "pe":
                for sem, v in deps:
                    eng.wait_ge(sem, v)
                last = None
                for f in r.fns:
                    last = f()
            else:
                for sem, v in deps[:-1]:
                    eng.wait_ge(sem, v)
                last = r.fns[0]()
                if deps:
                    last._wait_ge(deps[-1][0], deps[-1][1])
            last.then_inc(self.sem[e], 1)
            self.cnt[e] += 1
            r.ev = (self.sem[e], self.cnt[e])
        self.recs = []

    def barrier(self):
        self.flush()
        sp = self.E["sp"]
        for d in self.dbufs:
            if self.waited["sp"].get(id(d.dsem), 0) < 16 * d.dcnt:
                sp.wait_ge(d.dsem, 16 * d.dcnt)
                self.waited["sp"][id(d.dsem)] = 16 * d.dcnt
        for e in ("pe", "act", "dve", "pool"):
            if self.cnt[e] > self.waited["sp"].get(id(self.sem[e]), 0):
                sp.wait_ge(self.sem[e], self.cnt[e])
                self.waited["sp"][id(self.sem[e])] = self.cnt[e]
        sp.sem_inc(self.sem["sp"], 1)
        self.cnt["sp"] += 1
        for e in ("pe", "act", "dve", "pool"):
            self.E[e].wait_ge(self.sem["sp"], self.cnt["sp"])
            self.waited[e][id(self.sem["sp"])] = self.cnt["sp"]
            for d in self.dbufs:
                self.waited[e][id(d.dsem)] = 16 * d.dcnt
            for e2 in ("pe", "act", "dve", "pool"):
                self.waited[e][id(self.sem[e2])] = self.cnt[e2]

    def finish(self):
        self.flush()
        last = {}
        for sem, v in self.out_events:
            k = id(sem)
            self.semobj[k] = sem
            last[k] = max(last.get(k, 0), v)
        for k, v in last.items():
            self.E["sp"].wait_ge(self.semobj[k], v)


def build_program(NCH, do_sample=True):
    nc = bass.Bass("TRN2", target_bir_lowering=False)
    Buf.ALL = []
    S = Sched(nc)
    NA = 2 * NCH + 1
    NM = NCH + 1

    def din(name, shape, dt=F32):
        return nc.dram_tensor(name, list(shape), dt, kind="ExternalInput").ap()

    def dout(name, shape, dt=F32):
        return nc.dram_tensor(name, list(shape), dt, kind="ExternalOutput").ap()

    x_all = din("x_all", [NA * 128, D])
    p0_all = din("p0_all", [NA * 128, PLE])
    p1_all = din("p1_all", [NM * 128, PLE])
    w_in_attn = din("w_in_attn", [D, 2560])
    w_out_attn = din("w_out_attn", [D, D])
    w_in_ret = din("w_in_ret", [D, 6144])
    w_out_ret = din("w_out_ret", [RVW, D])
    w_ple = din("w_ple", [2, PLE, D])
    w_gate = din("w_gate", [2, D, D])
    norms = din("norms", [4, D])
    sinks = din("sinks", [1, NH])
    cache_k = din("cache_k", [NBS, 128, 256])
    cache_v = din("cache_v", [NBS, 128, 256])
    state_in = din("state_in", [NBS, RH, DK, DV])
    ropeA = din("ropeA", [NA, 128, 2, 32])
    ropeR = din("ropeR", [NA, 128, 2, 128])
    cst_bf = din("cst_bf", [128, 128 + 3 * 256 + 256])
    cst_smask = din("cst_smask", [NBS, 128, 256])
    cst_f = din("cst_f", [128, 16 + 256 + NBS * 4 + 32])

    y_p = dout("y_p", [NCH * 128, D])
    y_s = dout("y_s", [128, D])
    kwin_p = dout("kwin_p", [128, 256])
    vwin_p = dout("vwin_p", [128, 256])
    kwin_s = dout("kwin_s", [NBS, 128, 256])
    vwin_s = dout("vwin_s", [NBS, 128, 256])
    rst_p = dout("rst_p", [RH, DK, DV])
    rst_s = dout("rst_s", [NBS, RH, DK, DV])

    x1_scr = nc.dram_tensor("x1_scr", [NA * 128, D], F32, kind="Internal").ap()
    og_scr = nc.dram_tensor("og_scr", [NM * 128, RVW], BF16, kind="Internal").ap()

    WXt = nc.alloc_sbuf_tensor("WX", [128, 8 * 6144], BF16)
    ARENA = 107 * 1024
    ARt = nc.alloc_sbuf_tensor("AR", [128, ARENA // 2], BF16)
    FIXt = nc.alloc_sbuf_tensor("FIX", [128, 2304], BF16)

    def _view(base, off, shape, dt):
        nb = int(np.prod(shape[1:])) * (4 if dt == F32 else 2)
        ap = base[:, off // 2:(off + nb) // 2]
        if dt == F32:
            ap = ap.bitcast(F32)
        if len(shape) == 3:
            ap = ap.rearrange("p (a n) -> p a n", a=shape[1])
        elif len(shape) == 4:
            ap = ap.rearrange("p (a b n) -> p a b n", a=shape[1], b=shape[2])
        return ap, nb

    class T:
        def __init__(self, name, shape, dt):
            self.name, self.shape, self.dt = name, list(shape), dt
            self.t = None
            self.b = None
            self.dsem = None
            self.dcnt = 0

    def place(base, cap, spec, off0=0):
        off = off0
        for ent in spec:
            grp = ent if isinstance(ent, (tuple, list)) else (ent,)
            buf = Buf(grp[0].name)
            mx = 0
            for t_ in grp:
                t_.t, nb = _view(base, off, t_.shape, t_.dt)
                t_.b = buf
                mx = max(mx, nb)
            off += (mx + 63) // 64 * 64
        assert off <= cap, ("arena overflow", off, cap)
        return off

    WX = T("WX", [128, 8 * 6144], BF16)
    WX.t = WXt[:, :]
    WX.b = Buf("WX")
    A0_IN, A0_OUT, A0_GATE, A0_PLE = 0, 8 * 2560, 8 * 2560 + 8 * 1024, 8 * 2560 + 16 * 1024
    R2_OUT, R2_GATE, R2_PLE = 0, 16 * 1024, 24 * 1024
    wxb = [Buf("wx%d" % i) for i in range(8)]
    wob = [Buf("wo%d" % i) for i in range(8)]
    wgb = [Buf("wg%d" % i) for i in range(8)]
    wpb = [Buf("wp%d" % i) for i in range(8)]

    ident = T("ident", [128, 128], BF16)
    masks = T("masks", [128, 3, 256], BF16)
    snew = T("snew", [128, 256], BF16)
    cf = T("cf", [128, 16 + 256 + NBS * 4 + 32], F32)
    esink = T("esink", [128, NH], F32)
    ss = T("ss", [128, 4], F32)
    ssp = T("ssp", [128, 4], F32)
    bst = T("bst", [128, 4, 8], F32)
    den = T("den", [128, 2 * NH], F32)
    place(FIXt, 4608, [ident, masks, snew, cf, esink, ss, ssp, bst, den])

    xin = [T("xin%d" % i, [128, D], F32) for i in range(3)]
    pin = [T("pin%d" % i, [128, PLE], F32) for i in range(3)]
    rope_t = [T("rope%d" % i, [128, 2, 128], F32) for i in range(3)]
    gpost = T("gpost", [128, D], F32)
    gpre = T("gpre", [128, D], F32)
    smk8 = T("smk8", [128, 16], BF16)
    junk = T("junk", [128, D], BF16)
    junk2 = T("junk2", [128, D], BF16)
    qTd = [T("qTd%d" % i, [128, 8, 128], BF16) for i in range(2)]
    sgA = [T("sgA%d" % i, [128, D], BF16) for i in range(2)]
    xn = T("xn", [128, D], BF16)
    hT = T("hT", [128, 8, 128], BF16)
    qk_f = T("qk_f", [128, 1280], F32)
    rtt = T("rtt", [128, 4, 512], F32)
    q_r = T("q_r", [128, D], BF16)
    k_r = T("k_r", [128, D], BF16)
    k_r2 = T("k_r2", [128, D], BF16)
    kf = T("kf", [128, 256], F32)
    vf = T("vf", [128, 256], F32)
    k_ext = T("k_ext", [128, 8, 128], BF16)
    kT_ext = [T("kT_ext%d" % i, [128, 8, 128], BF16) for i in range(3)]
    V_aug = [T("V_aug%d" % i, [128, NKV, 65], BF16) for i in range(3)]
    qT = T("qT", [128, 8, 128], BF16)
    kT = T("kT", [128, 8, 128], BF16)
    pT = [T("pT%d" % i, [128, 1024], BF16) for i in range(2)]
    sg = T("sg", [128, 2048], BF16)
    v_b = T("v_b", [128, 2048], BF16)
    on_f = T("on_f", [128, D], F32)
    og = T("og", [128, 2048], BF16)
    ogT = T("ogT", [128, 16, 128], BF16)
    tmp = T("tmp", [128, D], F32)
    xa = T("xa", [128, D], F32)
    xb = T("xb", [128, D], BF16)
    xT = T("xT", [128, 8, 128], BF16)
    sig = T("sig", [128, D], F32)
    pb = T("pb", [128, PLE], BF16)
    pTt = T("pTt", [128, 2, 128], BF16)
    xout = [T("xout%d" % i, [128, D], F32) for i in range(3)]
    AT = T("AT", [128, 4, 128], BF16)
    S_f = T("S_f", [128, RH, 2, DV], F32)
    S_b = T("S_b", [128, RH, 2, DV], BF16)
    ogin = [T("ogin%d" % i, [128, 2048], BF16) for i in range(2)]
    stat = T("stat", [128, 32], F32)
    kc_f = [T("kc_f%d" % i, [128, 256], F32) for i in range(2)]
    vc_f = [T("vc_f%d" % i, [128, 256], F32) for i in range(2)]
    smk = [T("smk%d" % i, [128, 256], BF16) for i in range(2)]
    NSS = 3
    Ss = [T("Ss%d" % i, [128, 2, DV], F32) for i in range(NSS)]
    Sb2 = [T("Sb2_%d" % i, [128, 2, DV], BF16) for i in range(2)]
    o_acc = T("o_acc", [128, 2048], F32)
    qTm = T("qTm", [128, 8, 128], BF16)
    k_rm = T("k_rm", [128, D], BF16)
    qTs = T("qTs", [128, 8, 128], BF16)
    k_rs = T("k_rs", [128, D], BF16)
    v_bs = T("v_bs", [128, 2048], BF16)
    sgs = T("sgs", [128, 2048], BF16)

    class RT:
        def __init__(self, i):
            self.i = i
        @property
        def t(self):
            return rtt.t[:, self.i, :]
        @property
        def b(self):
            return rtt.b
    rt = [RT(i) for i in range(4)]

    def layout_A0():
        used = place(ARt, ARENA, [gpost, gpre, smk8, xin[0], xin[1], pin[0], pin[1], rope_t[0], rope_t[1], (rtt, junk), xn, hT, qk_f,
                           q_r, kf, vf, k_ext, kT_ext[0], kT_ext[1], kT_ext[2], V_aug[0], V_aug[1], V_aug[2],
                           qTd[0], qTd[1], pT[0], pT[1], sgA[0], sgA[1], on_f,
                           (og, junk2), ogT, tmp, xa, xb, xT, sig, pb, pTt, xout[0], xout[1],
                           kc_f[0], kc_f[1], vc_f[0], vc_f[1], smk[0], smk[1]])
        place(WXt, 96 * 1024, [xin[2], pin[2], rope_t[2], xout[2]], off0=76 * 1024)

    def layout_R1():
        place(ARt, ARENA, [S_f, S_b, xin[0], xin[1], rope_t[0], rope_t[1], (rtt, junk), (xn, k_r2, k_rm), (hT, AT), (qk_f, og),
                           q_r, k_r, qT, kT, sg, v_b, stat] +
              ([Ss[i] for i in range(NSS)] + [Sb2[0], Sb2[1], o_acc, qTm, qTs, k_rs, v_bs, sgs] if do_sample else []))

    def layout_R1p():
        off = place(ARt, ARENA, [S_f, S_b, gpre, xin[0], xin[1], rope_t[0], rope_t[1], (rtt, junk)])
        bind, off = make_sets([xn, hT, qk_f, k_r, k_r2, v_b], off)
        return bind

    def make_sets(tiles, off0, base=None, cap=None):
        base = ARt if base is None else base
        saved = []
        off = off0
        for par in range(2):
            ents = []
            for t_ in tiles:
                ap, nb = _view(base, off, t_.shape, t_.dt)
                ents.append((t_, ap, Buf("%s_s%d" % (t_.name, par))))
                off += (nb + 63) // 64 * 64
            saved.append(ents)
        assert off <= (ARENA if cap is None else cap), ("arena overflow (sets)", off)

        def bind(par):
            def f():
                for t_, ap, b_ in saved[par]:
                    t_.t = ap
                    t_.b = b_
            return f
        return bind, off

    def layout_R2():
        off = place(ARt, ARENA, [gpost, xin[0], xin[1], pin[0], pin[1], ogin[0], ogin[1], xout[0], xout[1]])
        bind, off = make_sets([ogT, junk2, tmp, xa, xb, xT, sig, pb, pTt, ssp], off)
        return bind

    PS = nc.alloc_psum_tensor("PS", [128, 8, 512], F32)
    psb = [Buf("ps%d" % i) for i in range(8)]

    def ps_bf(bank):
        return PS[:, bank, :].bitcast(BF16)

    op, pe, dma = S.op, S.pe, S.dma
    V, ACT, POOL = nc.vector, nc.scalar, nc.gpsimd
    PEe = nc.tensor
    DEC_P, DEC_S, CM_P, CM_S, RM8, GCOL = 0, 8, 16, 144, 272, 272 + NBS * 4

    def barrier():
        S.barrier()
        for b_ in Buf.ALL:
            b_.w = None
            b_.r = []

    dma("pool", ident.t[:, :], cst_bf[:, 0:128], writes=[ident.b], sbuf=ident)
    dma("pool", masks.t[:, :, :].rearrange("p a n -> p (a n)"), cst_bf[:, 128:128 + 768], writes=[masks.b], sbuf=masks)
    dma("pool", snew.t[:, :], cst_bf[:, 896:896 + 256], writes=[snew.b], sbuf=snew)
    dma("sp", cf.t[:, :], cst_f, writes=[cf.b], sbuf=cf)
    dma("sp", esink.t[:, :], sinks.rearrange("a n -> (a n)").partition_broadcast(128), writes=[esink.b], sbuf=esink)
    op("act", lambda: ACT.activation(out=esink.t[:, :], in_=esink.t[:, :], func=AF.Exp), reads=[esink.b], writes=[esink.b])

    def load_w(dst_off, src_, K, N, bufs, blocks=None, late_bufs=None):
        def one(kt, c0, bb):
            c1 = min(N, c0 + 2048)
            dma("pool", WX.t[:, dst_off + kt * N + c0: dst_off + kt * N + c1],
                src_[kt * 128:(kt + 1) * 128, c0:c1], writes=[bb[kt % 8]], sbuf=bb[kt % 8], waw=False)
        late = []
        for kt in range(K // 128):
            for c0 in range(0, N, 2048):
                if late_bufs is not None and c0 >= 4096:
                    late.append((kt, c0))
                else:
                    one(kt, c0, bufs)
        for kt, c0 in late:
            one(kt, c0, late_bufs)

    def load_gpost(i):
        dma("sp", gpost.t[:, :], norms[i].partition_broadcast(128), writes=[gpost.b], sbuf=gpost)

    def transpose_to(src, ntile, bank, dst, evac_eng, scale_col0=None):
        src2 = src.t if len(src.shape) == 2 else src.t.rearrange("p a n -> p (a n)")
        for g0 in range(0, ntile, 8):
            n = min(8, ntile - g0)
            bk = bank + g0 // 8
            pe([lambda t=t, bk=bk, g0=g0: PEe.transpose(out=ps_bf(bk)[:, (t - g0) * 128:(t - g0 + 1) * 128],
                                                       in_=src2[:, t * 128:(t + 1) * 128], identity=ident.t[:, :])
                for t in range(g0, g0 + n)],
               reads=[src.b, ident.b], writes=[psb[bk]])
            if scale_col0 is not None:
                for t in range(g0, g0 + n):
                    op("act", lambda t=t, bk=bk, g0=g0: ACT.activation(
                        out=dst.t[:, t, :], in_=ps_bf(bk)[:, (t - g0) * 128:(t - g0 + 1) * 128], func=AF.Copy,
                        scale=cf.t[:, scale_col0 + t:scale_col0 + t + 1]),
                       reads=[psb[bk], cf.b], writes=[dst.b])
                continue
            dst_ap = dst.t[:, g0:g0 + n, :].rearrange("p a n -> p (a n)")
            if evac_eng == "act":
                op("act", lambda bk=bk, n=n, dst_ap=dst_ap: ACT.copy(out=dst_ap, in_=ps_bf(bk)[:, 0:n * 128]),
                   reads=[psb[bk]], writes=[dst.b])
            else:
                op("dve", lambda bk=bk, n=n, dst_ap=dst_ap: V.tensor_copy(out=dst_ap, in_=ps_bf(bk)[:, 0:n * 128]),
                   reads=[psb[bk]], writes=[dst.b])

    USE_GPRE = [False]

    def load_gpre(i):
        dma("sp", gpre.t[:, :], norms[i].partition_broadcast(128), writes=[gpre.b], sbuf=gpre)

    def prenorm_and_hT(xt, gi, tbank):
        op("act", lambda: ACT.activation(out=junk.t[:, :], in_=xt.t[:, :], func=AF.Square, accum_out=ss.t[:, 0:1]),
           reads=[xt.b], writes=[junk.b, ss.b])
        op("dve", lambda: V.tensor_scalar(out=ss.t[:, 1:2], in0=ss.t[:, 0:1], scalar1=1.0 / D, scalar2=EPS,
                                          op0=ALU.mult, op1=ALU.add), reads=[ss.b], writes=[ss.b])
        op("act", lambda: ACT.activation(out=ss.t[:, 1:2], in_=ss.t[:, 1:2], func=AF.Sqrt), reads=[ss.b], writes=[ss.b])
        op("dve", lambda: V.reciprocal(out=ss.t[:, 1:2], in_=ss.t[:, 1:2]), reads=[ss.b], writes=[ss.b])
        if USE_GPRE[0]:
            op("dve", lambda: V.tensor_tensor(out=xn.t[:, :], in0=xt.t[:, :], in1=gpre.t[:, :], op=ALU.mult),
               reads=[xt.b, gpre.b], writes=[xn.b], n=1024)
            transpose_to(xn, 8, tbank, hT, "act")
        else:
            op("dve", lambda: V.tensor_copy(out=xn.t[:, :], in_=xt.t[:, :]), reads=[xt.b], writes=[xn.b], n=1024)
            transpose_to(xn, 8, tbank, hT, "act", scale_col0=GCOL + 8 * gi)

    SPLIT = [False]

    def proj(lhs, nkt, woff, wn, c0, ncols, bank0, wb=None):
        nb = (ncols + 511) // 512
        wb = wxb if wb is None else wb
        groups = []
        for kt in range(nkt):
            fns = []
            for j in range(nb):
                cc0 = c0 + j * 512
                cc1 = min(c0 + ncols, cc0 + 512)
                fns.append(lambda kt=kt, j=j, cc0=cc0, cc1=cc1: PEe.matmul(
                    PS[:, bank0 + j, 0:cc1 - cc0], lhsT=lhs.t[:, kt, :],
                    rhs=WX.t[:, woff + kt * wn + cc0: woff + kt * wn + cc1],
                    start=(kt == 0), stop=(kt == nkt - 1)))
            groups.append((kt, fns))
        wr = [psb[bank0 + j] for j in range(nb)]
        if SPLIT[0]:
            for kt, fns in groups:
                pe(fns, reads=[lhs.b, wb[kt % 8]], writes=wr, ncols=512)
        else:
            allf = [f for _, fns in groups for f in fns]
            pe(allf, reads=[lhs.b] + [wb[k % 8] for k in range(min(nkt, 8))], writes=wr, ncols=512)

    def resid_tail(xt, ybank, pt, woff_gate, woff_ple, out_ap, is_out, oslot, tbx=None, tbp=None, gb=None, pleb=None):
        tbx = ybank + 2 if tbx is None else tbx
        tbp = ybank + 3 if tbp is None else tbp
        gb = ybank + 4 if gb is None else gb
        pleb = ybank if pleb is None else pleb
        ss = ssp
        yb = [psb[ybank], psb[ybank + 1]]
        yap = PS[:, ybank:ybank + 2, :].rearrange("p a n -> p (a n)")
        op("act", lambda: ACT.activation(out=junk2.t[:, :], in_=yap, func=AF.Square, accum_out=ss.t[:, 2:3]),
           reads=yb, writes=[junk2.b, ss.b])
        op("dve", lambda: V.tensor_scalar(out=ss.t[:, 3:4], in0=ss.t[:, 2:3], scalar1=1.0 / D, scalar2=EPS,
                                          op0=ALU.mult, op1=ALU.add), reads=[ss.b], writes=[ss.b])
        op("act", lambda: ACT.activation(out=ss.t[:, 3:4], in_=ss.t[:, 3:4], func=AF.Sqrt), reads=[ss.b], writes=[ss.b])
        op("dve", lambda: V.reciprocal(out=ss.t[:, 3:4], in_=ss.t[:, 3:4]), reads=[ss.b], writes=[ss.b])
        op("dve", lambda: V.scalar_tensor_tensor(out=tmp.t[:, :], in0=yap, scalar=ss.t[:, 3:4],
                                                 in1=gpost.t[:, :], op0=ALU.mult, op1=ALU.mult),
           reads=yb + [ss.b, gpost.b], writes=[tmp.b])
        op("dve", lambda: V.tensor_tensor(out=xa.t[:, :], in0=xt.t[:, :], in1=tmp.t[:, :], op=ALU.add),
           reads=[xt.b, tmp.b], writes=[xa.b], n=1024)
        op("act", lambda: ACT.copy(out=xb.t[:, :], in_=xa.t[:, :]), reads=[xa.b], writes=[xb.b])
        op("act", lambda: ACT.copy(out=pb.t[:, :], in_=pt.t[:, :]), reads=[pt.b], writes=[pb.b])
        transpose_to(xb, 8, tbx, xT, "dve")
        transpose_to(pb, 2, tbp, pTt, "dve")
        proj(xT, 8, woff_gate, D, 0, D, gb, wb=wgb)
        op("act", lambda: ACT.activation(out=sig.t[:, :], in_=PS[:, gb:gb + 2, :].rearrange("p a n -> p (a n)"),
                                         func=AF.Tanh, scale=0.5), reads=[psb[gb], psb[gb + 1]], writes=[sig.b], n=1024)
        proj(pTt, 2, woff_ple, D, 0, D, pleb, wb=wpb)
        pb_ = [psb[pleb], psb[pleb + 1]]
        pap = PS[:, pleb:pleb + 2, :].rearrange("p a n -> p (a n)")
        op("dve", lambda: V.scalar_tensor_tensor(out=tmp.t[:, :], in0=sig.t[:, :], scalar=1.0, in1=pap,
                                                 op0=ALU.add, op1=ALU.mult),
           reads=pb_ + [sig.b], writes=[tmp.b], n=1024)
        xo = xout[oslot]
        op("dve", lambda: V.scalar_tensor_tensor(out=xo.t[:, :], in0=tmp.t[:, :], scalar=0.5, in1=xa.t[:, :],
                                                 op0=ALU.mult, op1=ALU.add),
           reads=[xa.b, tmp.b], writes=[xo.b], n=1024)
        dma("sp", out_ap, xo.t[:, :], reads=[xo.b], sbuf=xo, is_out=is_out)

    def do_rope(srcT, src_ap, dstT, dst_ap, nh, half, cst, eng):
        E_ = V if eng == "dve" else POOL
        sv = src_ap.rearrange("p (h t f) -> p h t f", h=nh, t=2)
        dv = dst_ap.rearrange("p (h t f) -> p h t f", h=nh, t=2)
        x1, x2 = sv[:, :, 0, :], sv[:, :, 1, :]
        cos = cst.t[:, 0:1, 0:half].to_broadcast([128, nh, half])
        sin = cst.t[:, 1:2, 0:half].to_broadcast([128, nh, half])
        n = nh * half

        def v3(i):
            return rtt.t[:, i, 0:n].rearrange("p (h f) -> p h f", h=nh)
        rd = [srcT.b, cst.b]
        op(eng, lambda: E_.tensor_tensor(out=v3(0), in0=x1, in1=cos, op=ALU.mult), reads=rd, writes=[rtt.b])
        op(eng, lambda: E_.tensor_tensor(out=v3(1), in0=x2, in1=sin, op=ALU.mult), reads=rd, writes=[rtt.b])
        op(eng, lambda: E_.tensor_tensor(out=v3(2), in0=x2, in1=cos, op=ALU.mult), reads=rd, writes=[rtt.b])
        op(eng, lambda: E_.tensor_tensor(out=v3(3), in0=x1, in1=sin, op=ALU.mult), reads=rd, writes=[rtt.b])
        op(eng, lambda: E_.tensor_tensor(out=dv[:, :, 0, :], in0=v3(0), in1=v3(1), op=ALU.subtract),
           reads=[rtt.b], writes=[dstT.b])
        op(eng, lambda: E_.tensor_tensor(out=dv[:, :, 1, :], in0=v3(2), in1=v3(3), op=ALU.add),
           reads=[rtt.b], writes=[dstT.b])

    def a0_load(ci):
        s = ci % 3
        dma("sp", xin[s].t[:, :], x_all[ci * 128:(ci + 1) * 128, :], writes=[xin[s].b], sbuf=xin[s])
        dma("sp", pin[s].t[:, :], p0_all[ci * 128:(ci + 1) * 128, :], writes=[pin[s].b], sbuf=pin[s])
        dma("sp", rope_t[s].t[:, :, 0:32], ropeA[ci], writes=[rope_t[s].b], sbuf=rope_t[s])

    def kext_fill(src_tile):
        kfv = src_tile.t[:, :].rearrange("p (k d) -> p k d", k=NKV)
        ke = k_ext.t[:, :, :].rearrange("p (k v) n -> p k v n", v=2)
        op("pool", lambda: POOL.tensor_copy(out=ke[:, :, 0, 0:64], in_=kfv), reads=[src_tile.b], writes=[k_ext.b])
        op("pool", lambda: POOL.tensor_copy(out=ke[:, :, 1, 64:128], in_=kfv), reads=[src_tile.b], writes=[k_ext.b])

    def a0_chunk(ci):
        s = ci % 3
        is_sample = (ci == NA - 1)
        xt, pt, cst = xin[s], pin[s], rope_t[s]
        cur, prv = ci % 3, (ci + 2) % 3
        qT_, sg_ = qTd[ci % 2], sgA[ci % 2]
        prenorm_and_hT(xt, 0, 0)
        rstd = ss.t[:, 1:2]
        proj(hT, 8, A0_IN, 2560, 1024, 512, 1)
        op("act", lambda: ACT.activation(out=qk_f.t[:, D:D + 256], in_=PS[:, 1, 0:256], func=AF.Copy, scale=rstd),
           reads=[psb[1], ss.b], writes=[qk_f.b], n=256)
        op("act", lambda: ACT.activation(out=vf.t[:, :], in_=PS[:, 1, 256:512], func=AF.Copy, scale=rstd),
           reads=[psb[1], ss.b], writes=[vf.b], n=256)
        for r_ in range(2):
            proj(hT, 8, A0_IN, 2560, 512 * r_, 512, r_)
            op("act", lambda r_=r_: ACT.activation(out=qk_f.t[:, 512 * r_:512 * (r_ + 1)], in_=PS[:, r_, :],
                                                   func=AF.Copy, scale=rstd), reads=[psb[r_], ss.b], writes=[qk_f.b], n=512)
        for r_ in range(2):
            proj(hT, 8, A0_IN, 2560, 1536 + 512 * r_, 512, r_)
            op("act", lambda r_=r_: ACT.activation(out=sg_.t[:, 512 * r_:512 * (r_ + 1)], in_=PS[:, r_, :],
                                                   func=AF.Silu, scale=rstd), reads=[psb[r_], ss.b], writes=[sg_.b], n=512)
        do_rope(qk_f, qk_f.t[:, 0:D], q_r, q_r.t[:, :], NH, 32, cst, "dve")
        do_rope(qk_f, qk_f.t[:, D:D + 256], kf, kf.t[:, :], NKV, 32, cst, "dve")
        kext_fill(kf)
        op("pool", lambda: POOL.tensor_copy(out=V_aug[cur].t[:, :, 0:64], in_=vf.t[:, :].rearrange("p (k d) -> p k d", k=NKV)),
           reads=[vf.b], writes=[V_aug[cur].b])
        if ci == 2 * NCH - 1:
            dma("sp", kwin_p, kf.t[:, :], reads=[kf.b], sbuf=kf, is_out=True)
            dma("sp", vwin_p, vf.t[:, :], reads=[vf.b], sbuf=vf, is_out=True)
        if is_sample:
            wh = [Buf("wst%d" % i) for i in range(8)]
            for b in range(NBS):
                dma("sp", kwin_s[b, 120:128, :], kf.t[b * LS:(b + 1) * LS, :], reads=[kf.b], sbuf=wh[b % 4], is_out=True)
                dma("sp", vwin_s[b, 120:128, :], vf.t[b * LS:(b + 1) * LS, :], reads=[vf.b], sbuf=wh[4 + b % 4], is_out=True)
        transpose_to(q_r, 8, 0, qT_, "act")
        transpose_to(k_ext, 8, 1, kT_ext[cur], "dve")
        if not is_sample:
            has_prev = not (ci == 0)
            mprev = 2 if ci == NCH else 1
            blocks = ([(kT_ext[prv], V_aug[prv], masks.t[:, mprev, :], masks.b)] if has_prev else []) + \
                     [(kT_ext[cur], V_aug[cur], masks.t[:, 0, :], masks.b)]
            nb_ = len(blocks)
            for kvh in range(NKV):
                for bi, (kt_, va_, mk, mkb) in enumerate(blocks):
                    attend_block(kvh, qT_, kt_, va_, mk, mkb, 2 + bi, pT[kvh % 2], bi * 512, bi == 0, bi == nb_ - 1)
                normalize(kvh)
        else:
            a0_attend_sample(cur, qT_)
        op("dve", lambda: V.tensor_tensor(out=og.t[:, 0:D], in0=on_f.t[:, :], in1=sg_.t[:, :], op=ALU.mult),
           reads=[on_f.b, sg_.b], writes=[og.b], n=1024)
        transpose_to(og, 8, 5, ogT, "act")
        proj(ogT, 8, A0_OUT, D, 0, D, 6, wb=wob)
        resid_tail(xt, 6, pt, A0_GATE, A0_PLE, x1_scr[ci * 128:(ci + 1) * 128, :], False, s, tbx=5, tbp=5, gb=6, pleb=6)

    OB = [lambda kvh: 4]

    def normalize(kvh):
        ob = OB[0](kvh)
        ov = PS[:, ob, 0:260].rearrange("p (g d) -> p g d", g=4)
        op("dve", lambda: V.tensor_tensor(out=den.t[:, 4 * kvh:4 * kvh + 4], in0=ov[:, :, 64],
                                          in1=esink.t[:, 4 * kvh:4 * kvh + 4], op=ALU.add),
           reads=[psb[ob], esink.b], writes=[den.b], n=4)
        op("dve", lambda: V.reciprocal(out=den.t[:, NH + 4 * kvh:NH + 4 * kvh + 4], in_=den.t[:, 4 * kvh:4 * kvh + 4]),
           reads=[den.b], writes=[den.b], n=4)
        op("dve", lambda: V.tensor_tensor(
            out=on_f.t[:, kvh * 256:(kvh + 1) * 256].rearrange("p (g d) -> p g d", g=4), in0=ov[:, :, 0:64],
            in1=den.t[:, NH + 4 * kvh:NH + 4 * kvh + 4].unsqueeze(2).to_broadcast([128, 4, 64]), op=ALU.mult),
           reads=[psb[ob], den.b], writes=[on_f.b], n=256)

    def attend_block(kvh, qT_, kt_, va_, mk, mkb, sbank, ptile, pcol, first, last):
        ob = OB[0](kvh)
        fns = []
        for var in range(2):
            o_ = PS[:, sbank, var * 256:(var + 1) * 256]
            fns.append(lambda o_=o_, var=var: PEe.matmul(
                o_, lhsT=kt_.t[:, kvh * 2 + var, :],
                rhs=qT_.t[:, 2 * kvh:2 * kvh + 2, :].rearrange("p a n -> p (a n)"), start=True, stop=False))
            fns.append(lambda o_=o_: PEe.matmul(o_, lhsT=ident.t[:, :], rhs=mk, start=False, stop=True))
        pe(fns, reads=[kt_.b, qT_.b, ident.b, mkb], writes=[psb[sbank]], ncols=256)
        op("act", lambda: ACT.activation(out=ptile.t[:, pcol:pcol + 512], in_=PS[:, sbank, :], func=AF.Exp, scale=0.125),
           reads=[psb[sbank]], writes=[ptile.b], n=512)
        fns = []
        for var in range(2):
            for tl in range(2):
                g = 2 * tl + var
                c0 = pcol + var * 256 + tl * 128
                fns.append(lambda g=g, c0=c0: PEe.matmul(
                    PS[:, ob, g * 65:(g + 1) * 65], lhsT=ptile.t[:, c0:c0 + 128],
                    rhs=va_.t[:, kvh, :], start=(first and g == 0), stop=last, skip_group_check=True))
        pe(fns, reads=[ptile.b, va_.b], writes=[psb[ob]], ncols=65)

    def a0_attend_sample(cur, qT_):
        prv = (cur + 1) % 3
        OB[0] = lambda kvh: 4 + kvh
        for kvh in range(NKV):
            attend_block(kvh, qT_, kT_ext[cur], V_aug[cur], snew.t[:, :], snew.b, 2 + kvh % 2, pT[kvh % 2], 0, True, False)
        for i in range(2):
            op("dve", lambda i=i: V.memset(pT[i].t[:, 0:512], 0.0), writes=[pT[i].b], n=512)
        dma("pool", smk8.t[:, 0:8], cst_smask[0][:, 0:8], writes=[smk8.b], sbuf=smk8)
        dma("pool", smk8.t[:, 8:16], cst_smask[0][:, 0:8], writes=[smk8.b], sbuf=smk8)
        for b in range(NBS):
            sl = b % 2
            cols = slice(b * LS, (b + 1) * LS)
            dma("sp", kc_f[sl].t[:, :], cache_k[b], writes=[kc_f[sl].b], sbuf=kc_f[sl])
            dma("sp", vc_f[sl].t[:, :], cache_v[b], writes=[vc_f[sl].b], sbuf=vc_f[sl])
            kcv = kc_f[sl].t[:, :].rearrange("p (k d) -> p k d", k=NKV)
            ke = k_ext.t[:, :, :].rearrange("p (k v) n -> p k v n", v=2)
            op("dve", lambda kcv=kcv, ke=ke: V.tensor_copy(out=ke[:, :, 0, 0:64], in_=kcv), reads=[kc_f[sl].b], writes=[k_ext.b], n=256)
            op("dve", lambda kcv=kcv, ke=ke: V.tensor_copy(out=ke[:, :, 1, 64:128], in_=kcv), reads=[kc_f[sl].b], writes=[k_ext.b], n=256)
            op("act", lambda sl=sl: ACT.copy(out=V_aug[prv].t[:, :, 0:64],
                                             in_=vc_f[sl].t[:, :].rearrange("p (k d) -> p k d", k=NKV)),
               reads=[vc_f[sl].b], writes=[V_aug[prv].b], n=256)
            transpose_to(k_ext, 8, b % 2, kT_ext[prv], "dve")
            for kvh in range(NKV):
                sbank = 2 + kvh % 2
                ptile = pT[kvh % 2]
                ob = 4 + kvh
                fns = []
                for var in range(2):
                    o_ = PS[:, sbank, var * 16:(var + 1) * 16]
                    fns.append(lambda o_=o_, var=var, kvh=kvh, cols=cols: PEe.matmul(
                        o_, lhsT=kT_ext[prv].t[:, kvh * 2 + var, :], rhs=qT_.t[:, 2 * kvh:2 * kvh + 2, cols],
                        start=True, stop=False))
                    fns.append(lambda o_=o_: PEe.matmul(o_, lhsT=ident.t[:, :], rhs=smk8.t[:, :], start=False, stop=True))
                pe(fns, reads=[kT_ext[prv].b, qT_.b, ident.b, smk8.b], writes=[psb[sbank]], ncols=16)
                pz = ptile.t[:, 0:512].rearrange("p (a n) -> p a n", a=4)
                op("act", lambda sbank=sbank, pz=pz, cols=cols: ACT.activation(
                    out=pz[:, :, cols], in_=PS[:, sbank, 0:32].rearrange("p (a l) -> p a l", a=4), func=AF.Exp, scale=0.125),
                   reads=[psb[sbank]], writes=[ptile.b], n=32)
                fns = []
                for var in range(2):
                    for tl in range(2):
                        g = 2 * tl + var
                        c0 = var * 256 + tl * 128
                        fns.append(lambda g=g, c0=c0, kvh=kvh, ptile=ptile, ob=ob, b=b: PEe.matmul(
                            PS[:, ob, g * 65:(g + 1) * 65], lhsT=ptile.t[:, c0:c0 + 128], rhs=V_aug[prv].t[:, kvh, :],
                            start=False, stop=(b == NBS - 1), skip_group_check=True))
                pe(fns, reads=[ptile.b, V_aug[prv].b], writes=[psb[ob]], ncols=65)
                op("dve", lambda pz=pz, cols=cols: V.memset(pz[:, :, cols], 0.0), writes=[ptile.b], n=32)
        for kvh in range(NKV):
            normalize(kvh)

    def r1_slot(ci):
        return (ci % 2) if ci != NA - 1 else 1 - (NCH % 2)

    def r1_load(ci):
        s = r1_slot(ci)
        dma("sp", xin[s].t[:, :], x1_scr[ci * 128:(ci + 1) * 128, :], writes=[xin[s].b], sbuf=xin[s])
        dma("sp", rope_t[s].t[:, :, :], ropeR[ci], writes=[rope_t[s].b], sbuf=rope_t[s])

    G128 = [float((1.0 - 2.0 ** (-5.0 - h)) ** 128) for h in range(RH)]
    G8 = [float((1.0 - 2.0 ** (-5.0 - h)) ** 8) for h in range(RH)]

    def r1_front(ci, kv_only, is_sample, o_kr, o_qT, o_vb, o_sg):
        s = r1_slot(ci)
        xt, cst = xin[s], rope_t[s]
        prenorm_and_hT(xt, 1, 7)
        dec0 = DEC_S if is_sample else DEC_P
        op("dve", lambda: V.tensor_scalar(out=bst.t[:, 0, :], in0=cf.t[:, dec0:dec0 + 8], scalar1=ss.t[:, 1:2],
                                          scalar2=None, op0=ALU.mult), reads=[cf.b, ss.b], writes=[bst.b])
        if not kv_only:
            proj(hT, 8, 0, 6144, 0, 2048, 0)
        else:
            proj(hT, 8, 0, 6144, 1024, 1024, 2)
        proj(hT, 8, 0, 6144, 2048, 2048, 4)
        if not kv_only:
            for j in range(4):
                op("act", lambda j=j: ACT.activation(out=qk_f.t[:, j * 256:(j + 1) * 256],
                                                     in_=PS[:, j // 2, (j % 2) * 256:(j % 2 + 1) * 256],
                                                     func=AF.Copy, scale=bst.t[:, 0, j:j + 1]),
                   reads=[psb[j // 2], bst.b], writes=[qk_f.b])
            do_rope(qk_f, qk_f.t[:, 0:D], q_r, q_r.t[:, :], RH, 128, cst, "dve")
        for j in range(4, 8):
            op("act", lambda j=j: ACT.activation(out=qk_f.t[:, (j - 4) * 256:(j - 3) * 256],
                                                 in_=PS[:, j // 2, (j % 2) * 256:(j % 2 + 1) * 256],
                                                 func=AF.Copy, scale=bst.t[:, 0, j:j + 1]),
               reads=[psb[j // 2], bst.b], writes=[qk_f.b])
        do_rope(qk_f, qk_f.t[:, 0:D], o_kr, o_kr.t[:, :], RH, 128, cst, "dve")
        op("act", lambda: ACT.activation(out=o_vb.t[:, :], in_=PS[:, 4:8, :].rearrange("p a n -> p (a n)"),
                                         func=AF.Copy, scale=ss.t[:, 1:2]),
           reads=[psb[4], psb[5], psb[6], psb[7], ss.b], writes=[o_vb.b])
        if not kv_only:
            proj(hT, 8, 0, 6144, 4096, 2048, 0, wb=wgb)
            op("act", lambda: ACT.activation(out=o_sg.t[:, :], in_=PS[:, 0:4, :].rearrange("p a n -> p (a n)"),
                                             func=AF.Silu, scale=ss.t[:, 1:2]),
               reads=[psb[0], psb[1], psb[2], psb[3], ss.b], writes=[o_sg.b])
            transpose_to(q_r, 8, 4, o_qT, "act")
            transpose_to(o_kr, 8, 5, kT, "dve")

    def r1_AT(cm, qTt):
        fns = []
        for h in range(RH):
            for j in range(2):
                fns.append(lambda h=h, j=j: PEe.matmul(PS[:, 6, h * 128:(h + 1) * 128], lhsT=kT.t[:, 2 * h + j, :],
                                                       rhs=qTt.t[:, 2 * h + j, :], start=(j == 0), stop=(j == 1)))
        pe(fns, reads=[kT.b, qTt.b], writes=[psb[6]])
        op("dve", lambda: V.tensor_tensor(out=AT.t[:, :, :], in0=PS[:, 6, :].rearrange("p (h n) -> p h n", h=4),
                                          in1=cf.t[:, cm:cm + 128].unsqueeze(1).to_broadcast([128, 4, 128]),
                                          op=ALU.mult), reads=[psb[6], cf.b], writes=[AT.b])

    def r1_chunk(ci, kv_only):
        r1_front(ci, kv_only, False, k_r, qT, v_b, sg)
        if not kv_only:
            r1_AT(CM_P, qT)
            for h in range(RH):
                fns = [lambda h=h: PEe.matmul(PS[:, h, :], lhsT=AT.t[:, h, :], rhs=v_b.t[:, h * 512:(h + 1) * 512],
                                              start=True, stop=False)]
                for j in range(2):
                    fns.append(lambda h=h, j=j: PEe.matmul(PS[:, h, :], lhsT=qT.t[:, 2 * h + j, :], rhs=S_b.t[:, h, j, :],
                                                           start=False, stop=(j == 1)))
                pe(fns, reads=[AT.b, v_b.b, qT.b, S_b.b], writes=[psb[h]], ncols=512)
        for h in range(RH):
            op("dve", lambda h=h: V.tensor_scalar(out=k_r2.t[:, h * 256:(h + 1) * 256], in0=k_r.t[:, h * 256:(h + 1) * 256],
                                                  scalar1=G128[h], scalar2=None, op0=ALU.mult),
               reads=[k_r.b], writes=[k_r2.b], n=256)
        for h in range(RH):
            for j in range(2):
                bk = (j if kv_only else 4 + (2 * h + j) % 4)
                pe([lambda h=h, j=j, bk=bk: PEe.matmul(PS[:, bk, :], lhsT=k_r2.t[:, (2 * h + j) * 128:(2 * h + j + 1) * 128],
                                                       rhs=v_b.t[:, h * 512:(h + 1) * 512], start=True, stop=True)],
                   reads=[k_r2.b, v_b.b], writes=[psb[bk]], ncols=512)
                op("dve", lambda h=h, j=j, bk=bk: V.scalar_tensor_tensor(
                    out=S_f.t[:, h, j, :], in0=S_f.t[:, h, j, :], scalar=G128[h], in1=PS[:, bk, :],
                    op0=ALU.mult, op1=ALU.add), reads=[psb[bk], S_f.b, S_b.b], writes=[S_f.b])
        if (not kv_only) or ci == NCH - 1:
            op("act", lambda: ACT.copy(out=S_b.t[:, :, :, :].rearrange("p h j n -> p (h j n)"),
                                       in_=S_f.t[:, :, :, :].rearrange("p h j n -> p (h j n)")),
               reads=[S_f.b], writes=[S_b.b], n=4096)
        if not kv_only:
            r1_finish(ci, [psb[0], psb[1], psb[2], psb[3]], PS[:, 0:4, :], sg)

    def r1_finish(ci, obufs, oap, sgt):
        rb = (lambda h: [obufs[h]]) if obufs else (lambda h: [o_acc.b])
        for h in range(RH):
            op("dve", lambda h=h: V.bn_stats(out=stat.t[:, h * 8:h * 8 + 6], in_=oap[:, h, :]), reads=rb(h), writes=[stat.b])
            op("dve", lambda h=h: V.bn_aggr(out=bst.t[:, 2, 2 * h:2 * h + 2], in_=stat.t[:, h * 8:h * 8 + 6]),
               reads=[stat.b], writes=[bst.b])
        mv = bst.t[:, 2, :].rearrange("p (h t) -> p h t", t=2)
        op("dve", lambda: V.tensor_scalar(out=bst.t[:, 3, 0:4], in0=mv[:, :, 1], scalar1=EPS, scalar2=None,
                                          op0=ALU.add), reads=[bst.b], writes=[bst.b])
        op("act", lambda: ACT.activation(out=bst.t[:, 3, 0:4], in_=bst.t[:, 3, 0:4], func=AF.Sqrt),
           reads=[bst.b], writes=[bst.b])
        op("dve", lambda: V.reciprocal(out=bst.t[:, 3, 0:4], in_=bst.t[:, 3, 0:4]), reads=[bst.b], writes=[bst.b])
        op("dve", lambda: V.scalar_tensor_tensor(out=bst.t[:, 3, 4:8], in0=mv[:, :, 0], scalar=-1.0, in1=bst.t[:, 3, 0:4],
                                                 op0=ALU.mult, op1=ALU.mult), reads=[bst.b], writes=[bst.b])
        for h in range(RH):
            op("act", lambda h=h: ACT.activation(out=og.t[:, h * 512:(h + 1) * 512], in_=oap[:, h, :], func=AF.Identity,
                                                 scale=bst.t[:, 3, h:h + 1], bias=bst.t[:, 3, 4 + h:5 + h]),
               reads=rb(h) + [bst.b], writes=[og.b])
        op("dve", lambda: V.tensor_tensor(out=og.t[:, :], in0=og.t[:, :], in1=sgt.t[:, :], op=ALU.mult),
           reads=[og.b, sgt.b], writes=[og.b], n=2048)
        mi = ci - NCH
        dma("pool", og_scr[mi * 128:(mi + 1) * 128, :], og.t[:, :], reads=[og.b], sbuf=OGST[0])

    def r1_sample_front(ci):
        r1_front(ci, False, True, k_rs, qTs, v_bs, sgs)
        r1_AT(CM_S, qTs)
        for h in range(RH):
            pe([lambda h=h: PEe.matmul(PS[:, h, :], lhsT=AT.t[:, h, :], rhs=v_bs.t[:, h * 512:(h + 1) * 512],
                                       start=True, stop=True)], reads=[AT.b, v_bs.b], writes=[psb[h]])
        op("act", lambda: ACT.copy(out=o_acc.t[:, :], in_=PS[:, 0:4, :].rearrange("p a n -> p (a n)")),
           reads=[psb[0], psb[1], psb[2], psb[3]], writes=[o_acc.b])

    SST = [Buf("sst%d" % i) for i in range(NSS)]
    OGST = [Buf("ogst")]

    def state_load(idx):
        b, h = idx // RH, idx % RH
        St = Ss[idx % NSS]
        dma("sp", St.t[:, :, :], state_in[b, h].rearrange("(j p) e -> p j e", p=128), writes=[St.b], sbuf=St)

    def r1_sample_b(b):
        cols = slice(b * LS, (b + 1) * LS)
        op("pool", lambda: POOL.tensor_copy(out=qTm.t[:, :, cols], in_=qTs.t[:, :, cols]), reads=[qTs.b], writes=[qTm.b])
        op("dve", lambda: V.tensor_tensor(out=k_rm.t[:, :].rearrange("p (h n) -> p h n", h=4),
                                              in0=k_rs.t[:, :].rearrange("p (h n) -> p h n", h=4),
                                              in1=cf.t[:, RM8 + 4 * b:RM8 + 4 * b + 4].unsqueeze(2).to_broadcast([128, 4, 256]),
                                              op=ALU.mult), reads=[k_rs.b, cf.b], writes=[k_rm.b])
        for h in range(RH):
            idx = b * RH + h
            if idx + 2 < NBS * RH:
                state_load(idx + 2)
            St, Sb_ = Ss[idx % NSS], Sb2[idx % 2]
            op("act", lambda St=St, Sb_=Sb_: ACT.copy(out=Sb_.t[:, :, :].rearrange("p j n -> p (j n)"),
                                                      in_=St.t[:, :, :].rearrange("p j n -> p (j n)")),
               reads=[St.b], writes=[Sb_.b])
            pe([lambda h=h, j=j, Sb_=Sb_: PEe.matmul(PS[:, h, :], lhsT=qTm.t[:, 2 * h + j, :], rhs=Sb_.t[:, j, :],
                                                    start=(j == 0), stop=(j == 1)) for j in range(2)],
               reads=[qTm.b, Sb_.b], writes=[psb[h]], ncols=512)
            op("dve", lambda h=h: V.tensor_tensor(out=o_acc.t[:, h * 512:(h + 1) * 512], in0=PS[:, h, :],
                                                  in1=o_acc.t[:, h * 512:(h + 1) * 512], op=ALU.add),
               reads=[psb[h], o_acc.b], writes=[o_acc.b])
            for j in range(2):
                bk = 4 + (2 * h + j) % 4
                pe([lambda h=h, j=j, bk=bk: PEe.matmul(PS[:, bk, :], lhsT=k_rm.t[:, (2 * h + j) * 128:(2 * h + j + 1) * 128],
                                                       rhs=v_bs.t[:, h * 512:(h + 1) * 512], start=True, stop=True)],
                   reads=[k_rm.b, v_bs.b], writes=[psb[bk]], ncols=512)
                op("dve", lambda h=h, j=j, bk=bk, St=St: V.scalar_tensor_tensor(
                    out=St.t[:, j, :], in0=St.t[:, j, :], scalar=G8[h], in1=PS[:, bk, :],
                    op0=ALU.mult, op1=ALU.add), reads=[psb[bk], St.b, Sb_.b], writes=[St.b])
            dma("pool", rst_s[b, h].rearrange("(j p) e -> p j e", p=128), St.t[:, :, :], reads=[St.b], sbuf=SST[idx % NSS], is_out=True)
        op("pool", lambda: POOL.memset(qTm.t[:, :, cols], 0.0), writes=[qTm.b])

    def r2_load(mi):
        s = mi % 2
        ci = NCH + mi
        dma("sp", xin[s].t[:, :], x1_scr[ci * 128:(ci + 1) * 128, :], writes=[xin[s].b], sbuf=xin[s])
        dma("sp", pin[s].t[:, :], p1_all[mi * 128:(mi + 1) * 128, :], writes=[pin[s].b], sbuf=pin[s])
        dma("sp", ogin[s].t[:, :], og_scr[mi * 128:(mi + 1) * 128, :], writes=[ogin[s].b], sbuf=ogin[s])

    def r2_chunk(mi):
        s = mi % 2
        transpose_to(ogin[s], 16, 0, ogT, "act")
        proj(ogT, 16, R2_OUT, D, 0, D, 2, wb=wob)
        out_ap = y_p[mi * 128:(mi + 1) * 128, :] if mi < NCH else y_s
        resid_tail(xin[s], 2, pin[s], R2_GATE, R2_PLE, out_ap, True, s, tbx=4, tbp=5, gb=6, pleb=4)

    layout_A0()
    load_w(A0_IN, w_in_attn, D, 2560, wxb)
    load_w(A0_OUT, w_out_attn, D, D, wob)
    load_w(A0_GATE, w_gate[0], D, D, wgb)
    load_w(A0_PLE, w_ple[0], PLE, D, wpb)
    load_gpost(2)
    load_gpre(0)
    USE_GPRE[0] = True
    for i in range(3):
        op("pool", lambda i=i: POOL.memset(V_aug[i].t[:, :, :], 1.0), writes=[V_aug[i].b])
    op("pool", lambda: POOL.memset(k_ext.t[:, :, :], 0.0), writes=[k_ext.b])
    nA = NA if do_sample else NA - 1
    a0_load(0)
    for ci in range(nA):
        if ci + 1 < nA:
            a0_load(ci + 1)
        SPLIT[0] = (ci == 0)
        a0_chunk(ci)
    SPLIT[0] = False
    if do_sample:
        dma("sp", kwin_s[:, 0:120, :], cache_k[:, 8:128, :], sbuf=Buf("d2d_k"), is_out=True)
        dma("sp", vwin_s[:, 0:120, :], cache_v[:, 8:128, :], sbuf=Buf("d2d_v"), is_out=True)
    barrier()

    bindR1p = layout_R1p()
    load_gpre(1)
    USE_GPRE[0] = True
    load_w(0, w_in_ret, D, 6144, wxb, late_bufs=wgb)
    op("dve", lambda: V.memset(S_f.t[:, :, :, :], 0.0), writes=[S_f.b])
    op("dve", lambda: V.memset(S_b.t[:, :, :, :], 0.0), writes=[S_b.b])
    r1_load(0)
    for ci in range(NCH):
        if ci + 1 < NCH:
            r1_load(ci + 1)
        S.ctx = bindR1p(ci % 2)
        S.ctx()
        SPLIT[0] = (ci == 0)
        r1_chunk(ci, True)
        SPLIT[0] = False
    S.ctx = None
    barrier()
    USE_GPRE[0] = False
    layout_R1()
    if do_sample:
        op("pool", lambda: POOL.memset(qTm.t[:, :, :], 0.0), writes=[qTm.b])
    order = list(range(NCH, 2 * NCH))
    if do_sample:
        sci = NA - 1
        r1_load(sci)
        state_load(0)
        state_load(1)
        r1_load(order[0])
        r1_sample_front(sci)
    else:
        r1_load(order[0])
    bper = (NBS + NCH - 1) // NCH
    nb_done = 0
    for k, ci in enumerate(order):
        if k + 1 < len(order):
            r1_load(order[k + 1])
        r1_chunk(ci, False)
        if do_sample:
            for _ in range(bper):
                if nb_done < NBS:
                    r1_sample_b(nb_done)
                    nb_done += 1
    dma("sp", rst_p.rearrange("h (j p) e -> p h j e", p=128), S_f.t[:, :, :, :], reads=[S_f.b], sbuf=S_f, is_out=True)
    if do_sample:
        r1_finish(NA - 1, None, o_acc.t[:, :].rearrange("p (h n) -> p h n", h=4), sgs)
    barrier()

    bindR2 = layout_R2()
    load_w(R2_OUT, w_out_ret, RVW, D, wob)
    load_w(R2_GATE, w_gate[1], D, D, wgb)
    load_w(R2_PLE, w_ple[1], PLE, D, wpb)
    load_gpost(3)
    nM = NM if do_sample else NM - 1
    r2_load(0)
    for mi in range(nM):
        if mi + 1 < nM:
            r2_load(mi + 1)
        S.ctx = bindR2(mi % 2)
        S.ctx()
        SPLIT[0] = (mi == 0)
        r2_chunk(mi)
        SPLIT[0] = False
    S.ctx = None
    S.finish()
    return nc


def _rope_tab(pos, half):
    inv = (np.float32(THETA) ** (-(np.arange(half, dtype=np.float32)) / np.float32(half))).astype(np.float32)
    ang = pos.astype(np.float32)[:, None] * inv[None, :]
    return np.stack([np.cos(ang), np.sin(ang)], axis=1).astype(np.float32)


def make_consts(NCH, half_idx):
    T = NCH * 128
    NA = 2 * NCH + 1
    pos_pref = (half_idx - 1) * T + np.arange(T)
    pos_main = half_idx * T + np.arange(T)
    pos_s = PAST + (np.arange(128) % LS)
    pos = np.concatenate([np.maximum(pos_pref, 0), pos_main, pos_s])
    ropeA = _rope_tab(pos, 32).reshape(NA, 128, 2, 32)
    ropeR = _rope_tab(pos, 128).reshape(NA, 128, 2, 128)
    k_ = np.arange(128)[:, None]
    q_ = np.arange(128)[None, :]
    maskO = np.where(q_ >= k_, 0.0, NEG).astype(np.float32)
    maskP = np.where(q_ < k_, 0.0, NEG).astype(np.float32)
    maskF = maskP if half_idx > 0 else np.full((128, 128), NEG, np.float32)
    ident = np.eye(128, dtype=np.float32)
    tb, tl = np.arange(128) // LS, np.arange(128) % LS
    smask = np.full((NBS, 128, 128), NEG, np.float32)
    for b in range(NBS):
        j = np.arange(128)[:, None]
        ok = (tb[None, :] == b) & (j > tl[None, :])
        smask[b] = np.where(ok, 0.0, NEG)
    snew = np.where((tb[:, None] == tb[None, :]) & (tl[:, None] <= tl[None, :]), 0.0, NEG).astype(np.float32)
    cst_bf = np.concatenate([ident] + [np.tile(m, (1, 2)) for m in (maskO, maskP, maskF, snew)], axis=1)
    cst_smask = np.tile(smask, (1, 1, 2))
    gam = 1.0 - 2.0 ** (-5.0 - np.arange(RH, dtype=np.float64))
    t = np.arange(128, dtype=np.float64)
    dec_p = np.concatenate([gam[None, :] ** (t[:, None] + 1), (gam[None, :] ** (-(t[:, None] + 1))) * DK ** -0.5], axis=1)
    l = (np.arange(128) % LS).astype(np.float64)
    dec_s = np.concatenate([gam[None, :] ** (l[:, None] + 1), (gam[None, :] ** (-(l[:, None] + 1))) * DK ** -0.5], axis=1)
    cm_p = (q_ >= k_).astype(np.float64)
    cm_s = ((tb[:, None] == tb[None, :]) & (tl[None, :] >= tl[:, None])).astype(np.float64)
    rm8 = np.zeros((128, NBS * 4))
    for b in range(NBS):
        rm8[:, 4 * b:4 * b + 4] = (tb[:, None] == b) * (gam[None, :] ** 8)
    cst_f = np.concatenate([dec_p, dec_s, cm_p, cm_s, rm8, np.zeros((128, 32))], axis=1).astype(np.float32)
    return dict(ropeA=ropeA, ropeR=ropeR, cst_bf=cst_bf.astype(np.float32), cst_smask=cst_smask.astype(np.float32),
                cst_f=cst_f)


def make_in_maps(inputs, NCH, n_cores):
    T = NCH * 128
    f = lambda a: np.ascontiguousarray(np.asarray(a, dtype=np.float32))
    xp, xs = f(inputs["x_prompt"]), f(inputs["x_sample"])
    pp, ps_ = f(inputs["p_prompt"]), f(inputs["p_sample"])
    shared = dict(
        w_in_attn=f(inputs["w_in_attn"])[0], w_out_attn=f(inputs["w_out_attn"])[0],
        w_in_ret=f(inputs["w_in_ret"])[0], w_out_ret=f(inputs["w_out_ret"])[0],
        w_ple=f(inputs["w_ple"]), w_gate=f(inputs["w_ple_gate"]),
        norms=np.concatenate([f(inputs["pre_norm"]), f(inputs["post_norm"])], axis=0),
        sinks=f(inputs["attn_sinks"]),
    )
    maps = []
    for c in range(n_cores):
        b, h = c // 2, c % 2
        own = slice(h * T, (h + 1) * T)
        prev = slice((h - 1) * T, h * T)
        sb_ = slice(c * NBS, (c + 1) * NBS)
        zx = np.zeros((T, D), np.float32)
        zp = np.zeros((T, PLE), np.float32)
        m = dict(shared)
        m["x_all"] = np.concatenate([xp[b, prev] if h else zx, xp[b, own], xs[sb_].reshape(128, D)], axis=0)
        m["p0_all"] = np.concatenate([pp[0, b, prev] if h else zp, pp[0, b, own], ps_[0, sb_].reshape(128, PLE)], axis=0)
        m["p1_all"] = np.concatenate([pp[1, b, own], ps_[1, sb_].reshape(128, PLE)], axis=0)
        m["cache_k"] = f(inputs["cache_k_win"])[0, sb_].reshape(NBS, 128, 256)
        m["cache_v"] = f(inputs["cache_v_win"])[0, sb_].reshape(NBS, 128, 256)
        m["state_in"] = f(inputs["state_ret"])[0, sb_]
        m.update(make_consts(NCH, h))
        cstf = m["cst_f"].copy()
        cstf[:, -32:] = shared["norms"].reshape(4, 8, 128).transpose(2, 0, 1).reshape(128, 32)
        m["cst_f"] = cstf
        maps.append(m)
    return maps


def assemble(results, NCH, n_cores):
    nb = n_cores // 2
    T = NCH * 128
    y_p = np.zeros((nb, 2 * T, D), np.float32)
    y_s = np.zeros((n_cores * NBS, LS, D), np.float32)
    kwp = np.zeros((1, nb, 128, NKV, HD), np.float32)
    vwp = np.zeros_like(kwp)
    kws = np.zeros((1, n_cores * NBS, 128, NKV, HD), np.float32)
    vws = np.zeros_like(kws)
    rsp = np.zeros((1, nb, RH, DK, DV), np.float32)
    rss = np.zeros((1, n_cores * NBS, RH, DK, DV), np.float32)
    for c in range(n_cores):
        r = results[c]
        b, h = c // 2, c % 2
        y_p[b, h * T:(h + 1) * T] = r["y_p"]
        y_s[c * NBS:(c + 1) * NBS] = np.asarray(r["y_s"]).reshape(NBS, LS, D)
        kws[0, c * NBS:(c + 1) * NBS] = np.asarray(r["kwin_s"]).reshape(NBS, 128, NKV, HD)
        vws[0, c * NBS:(c + 1) * NBS] = np.asarray(r["vwin_s"]).reshape(NBS, 128, NKV, HD)
        rss[0, c * NBS:(c + 1) * NBS] = r["rst_s"]
        if h == 1:
            kwp[0, b] = np.asarray(r["kwin_p"]).reshape(128, NKV, HD)
            vwp[0, b] = np.asarray(r["vwin_p"]).reshape(128, NKV, HD)
            rsp[0, b] = r["rst_p"]
    return (y_p, y_s, kwp, vwp, kws, vws, rsp, rss)


_NC_CACHE = {}


def kernel(**inputs):
    NCH, n_cores = 16, 8
    if NCH not in _NC_CACHE:
        _NC_CACHE[NCH] = build_program(NCH)
    nc = _NC_CACHE[NCH]
    in_maps = make_in_maps(inputs, NCH, n_cores)
    res = run_bass_kernel_spmd(nc, in_maps, core_ids=list(range(n_cores)))
    return assemble(res.results, NCH, n_cores)
```

```python
import numpy as np
import concourse.bass as bass
import concourse.mybir as mybir
from concourse.alu_op_type import AluOpType as ALU
from concourse.bass_utils import run_bass_kernel_spmd

F32 = mybir.dt.float32
BF16 = mybir.dt.bfloat16
AF = mybir.ActivationFunctionType

D = 1024
HD = 64
NH = 16
NKV = 4
PLE = 256
RH = 4
DK = 256
DV = 512
RVW = 2048
EPS = 1e-6
NEG = -30000.0
PAST = 16384
THETA = 10000.0
NBS = 16
LS = 8


class Buf:
    __slots__ = ("name", "w", "r", "dsem", "dcnt", "last_dma")

    ALL = []

    def __init__(self, name):
        self.name = name
        self.w = None
        self.r = []
        self.dsem = None
        self.dcnt = 0
        Buf.ALL.append(self)


class Rec:
    __slots__ = ("kind", "eng", "fns", "reads", "writes", "preds", "cost", "ev", "sbuf", "out", "in_", "kw",
                 "is_out", "fin", "ctx")


class Sched:
    def __init__(self, nc):
        self.nc = nc
        self.E = {"pe": nc.tensor, "act": nc.scalar, "dve": nc.vector, "pool": nc.gpsimd, "sp": nc.sync}
        self.sem = {e: nc.alloc_semaphore("cnt_" + e) for e in self.E}
        self.cnt = {e: 0 for e in self.E}
        self.waited = {e: {} for e in self.E}
        self.semobj = {}
        self.out_events = []
        self.nsem = 0
        self.dbufs = []
        self.recs = []
        self.reorder = True
        self.ctx = None

    def _record(self, r, reads, writes, waw=True):
        idx = len(self.recs)
        preds = set()
        for b in reads:
            if b.w is not None:
                preds.add(b.w)
        import os
        nw = os.environ.get("KSCHED_NOWAR", "")
        for b in writes:
            nowar = bool(nw) and (nw == "1" or any(b.name.startswith(p) for p in nw.split(",")))
            if waw and b.w is not None and not nowar:
                preds.add(b.w)
            if waw and not nowar:
                preds.update(b.r)
        r.preds = preds
        r.ev = None
        r.ctx = self.ctx
        self.recs.append(r)
        for b in reads:
            b.r.append(idx)
        for b in writes:
            b.w = idx
            b.r = []

    def op(self, e, fn, reads=(), writes=(), n=None):
        r = Rec()
        r.kind, r.eng, r.fns = "op", e, [fn]
        if e == "pool":
            r.cost = 0.5 + (n or 256) / 450.0
        elif e == "act":
            r.cost = 0.3 + (n or 512) / 800.0
        else:
            r.cost = 0.25 + (n or 256) / 1000.0
        self._record(r, reads, writes)

    def pe(self, fns, reads=(), writes=(), ncols=128):
        r = Rec()
        r.kind, r.eng, r.fns = "pe", "pe", list(fns)
        r.cost = 0.1 + len(fns) * (0.04 + ncols / 1950.0)
        self._record(r, reads, writes)

    def dma(self, q, out, in_, reads=(), writes=(), sbuf=None, is_out=False, waw=True, **kw):
        r = Rec()
        r.kind, r.eng, r.fns = "dma", q, None
        r.out, r.in_, r.kw, r.sbuf, r.is_out = out, in_, kw, sbuf, is_out
        try:
            nbytes = max(int(out.nbytes()), int(in_.nbytes()))
        except Exception:
            nbytes = 0
        r.cost = 2.0 + nbytes / 160e3
        self._record(r, reads, writes, waw)
        prev = getattr(sbuf, "last_dma", None)
        if prev is not None and prev[0] is self.recs:
            r.preds.add(prev[1])
        sbuf.last_dma = (self.recs, len(self.recs) - 1)

    def _order(self):
        import heapq
        recs = self.recs
        n = len(recs)
        if not self.reorder:
            return list(range(n))
        succ = [[] for _ in range(n)]
        npred = [0] * n
        for i, r in enumerate(recs):
            npred[i] = len(r.preds)
            for p in r.preds:
                succ[p].append(i)
        ready_t = [0.0] * n
        efree = {e: 0.0 for e in self.E}
        heap = [(0.0, i) for i in range(n) if npred[i] == 0]
        heapq.heapify(heap)
        order = []
        import os
        dbg = bool(os.environ.get("KSCHED_DEBUG"))
        crit, engprev, st_, last_on = {}, {}, {}, {}
        while heap:
            t, i = heapq.heappop(heap)
            r = recs[i]
            if r.kind == "dma":
                st = max(t, efree[r.eng])
                efree[r.eng] = st + 0.1
                fin = st + r.cost
            else:
                st = max(t, efree[r.eng])
                fin = st + r.cost
                efree[r.eng] = fin
            r.fin = fin
            if dbg:
                st_[i] = st
                engprev[i] = last_on.get(r.eng, -1) if st > t + 1e-9 else -1
                last_on[r.eng] = i
            order.append(i)
            for s_ in succ[i]:
                lat = 0.15 if recs[s_].eng != r.eng else 0.05
                if fin + lat > ready_t[s_]:
                    ready_t[s_] = fin + lat
                    if dbg:
                        crit[s_] = i
                npred[s_] -= 1
                if npred[s_] == 0:
                    heapq.heappush(heap, (ready_t[s_], s_))
        assert len(order) == n
        self.est_us = max(r.fin for r in recs) if recs else 0.0
        import os
        if os.environ.get("KSCHED_DEBUG"):
            busy = {}
            for r in recs:
                busy[r.eng] = busy.get(r.eng, 0.0) + (r.cost if r.kind != 'dma' else 0.1)
            print("SCHED n=%d est=%.1fus busy=%s" % (n, self.est_us, {k: round(v) for k, v in busy.items()}))
            i = max(range(n), key=lambda k: recs[k].fin)
            on_eng, via_dep, via_eng = {}, 0, 0
            while i >= 0:
                r = recs[i]
                on_eng[r.eng] = on_eng.get(r.eng, 0.0) + r.cost
                if engprev.get(i, -1) >= 0:
                    via_eng += 1
                    i = engprev[i]
                elif i in crit:
                    via_dep += 1
                    i = crit[i]
                else:
                    break
            print("   critical chain: time on engines", {k: round(v) for k, v in on_eng.items()}, "dep-hops", via_dep, "engine-queue-hops", via_eng)
        return order

    def _waits(self, e, preds):
        need = {}
        for p in preds:
            sem, val = self.recs[p].ev
            k = id(sem)
            self.semobj[k] = sem
            if need.get(k, 0) < val:
                need[k] = val
        out = []
        wd = self.waited[e]
        for k, v in need.items():
            if wd.get(k, 0) >= v:
                continue
            wd[k] = v
            out.append((self.semobj[k], v))
        return out

    def flush(self):
        for i in self._order():
            r = self.recs[i]
            e = r.eng
            eng = self.E[e]
            if r.ctx is not None:
                r.ctx()
            deps = self._waits(e, r.preds)
            if r.kind == "dma":
                for sem, v in deps:
                    eng.wait_ge(sem, v)
                sb_ = r.sbuf
                if sb_.dsem is None:
                    sb_.dsem = self.nc.alloc_semaphore("d_%s_%d" % (sb_.name, self.nsem))
                    self.nsem += 1
                    self.dbufs.append(sb_)
                sb_.dcnt += 1
                eng.dma_start(out=r.out, in_=r.in_, **r.kw).then_inc(sb_.dsem, 16)
                r.ev = (sb_.dsem, 16 * sb_.dcnt)
                if r.is_out:
                    self.out_events.append(r.ev)
                continue
            if r.kind == "pe":
                for sem, v in deps:
                    eng.wait_ge(sem, v)
                last = None
                for f in r.fns:
                    last = f()
            else:
                for sem, v in deps[:-1]:
                    eng.wait_ge(sem, v)
                last = r.fns[0]()
                if deps:
                    last._wait_ge(deps[-1][0], deps[-1][1])
            last.then_inc(self.sem[e], 1)
            self.cnt[e] += 1
            r.ev = (self.sem[e], self.cnt[e])
        self.recs = []

    def barrier(self):
        self.flush()
        sp = self.E["sp"]
        for d in self.dbufs:
            if self.waited["sp"].get(id(d.dsem), 0) < 16 * d.dcnt:
                sp.wait_ge(d.dsem, 16 * d.dcnt)
                self.waited["sp"][id(d.dsem)] = 16 * d.dcnt
        for e in ("pe", "act", "dve", "pool"):
            if self.cnt[e] > self.waited["sp"].get(id(self.sem[e]), 0):
                sp.wait_ge(self.sem[e], self.cnt[e])
                self.waited["sp"][id(self.sem[e])] = self.cnt[e]
        sp.sem_inc(self.sem["sp"], 1)
        self.cnt["sp"] += 1
        for e in ("pe", "act", "dve", "pool"):
            self.E[e].wait_ge(self.sem["sp"], self.cnt["sp"])
            self.waited[e][id(self.sem["sp"])] = self.cnt["sp"]
            for d in self.dbufs:
                self.waited[e][id(d.dsem)] = 16 * d.dcnt
            for e2 in ("pe", "act", "dve", "pool"):
                self.waited[e][id(self.sem[e2])] = self.cnt[e2]

    def finish(self):
        self.flush()
        last = {}
        for sem, v in self.out_events:
            k = id(sem)
            self.semobj[k] = sem
            last[k] = max(last.get(k, 0), v)
        for k, v in last.items():
            self.E["sp"].wait_ge(self.semobj[k], v)


def build_program(NCH, do_sample=True):
    nc = bass.Bass("TRN2", target_bir_lowering=False)
    Buf.ALL = []
    S = Sched(nc)
    NA = 2 * NCH + 1
    NM = NCH + 1

    def din(name, shape, dt=F32):
        return nc.dram_tensor(name, list(shape), dt, kind="ExternalInput").ap()

    def dout(name, shape, dt=F32):
        return nc.dram_tensor(name, list(shape), dt, kind="ExternalOutput").ap()

    x_all = din("x_all", [NA * 128, D])
    p0_all = din("p0_all", [NA * 128, PLE])
    p1_all = din("p1_all", [NM * 128, PLE])
    w_in_attn = din("w_in_attn", [D, 2560])
    w_out_attn = din("w_out_attn", [D, D])
    w_in_ret = din("w_in_ret", [D, 6144])
    w_out_ret = din("w_out_ret", [RVW, D])
    w_ple = din("w_ple", [2, PLE, D])
    w_gate = din("w_gate", [2, D, D])
    norms = din("norms", [4, D])
    sinks = din("sinks", [1, NH])
    cache_k = din("cache_k", [NBS, 128, 256])
    cache_v = din("cache_v", [NBS, 128, 256])
    state_in = din("state_in", [NBS, RH, DK, DV])
    ropeA = din("ropeA", [NA, 128, 2, 32])
    ropeR = din("ropeR", [NA, 128, 2, 128])
    cst_bf = din("cst_bf", [128, 128 + 3 * 256 + 256])
    cst_smask = din("cst_smask", [NBS, 128, 256])
    cst_f = din("cst_f", [128, 16 + 256 + NBS * 4 + 32])

    y_p = dout("y_p", [NCH * 128, D])
    y_s = dout("y_s", [128, D])
    kwin_p = dout("kwin_p", [128, 256])
    vwin_p = dout("vwin_p", [128, 256])
    kwin_s = dout("kwin_s", [NBS, 128, 256])
    vwin_s = dout("vwin_s", [NBS, 128, 256])
    rst_p = dout("rst_p", [RH, DK, DV])
    rst_s = dout("rst_s", [NBS, RH, DK, DV])

    x1_scr = nc.dram_tensor("x1_scr", [NA * 128, D], F32, kind="Internal").ap()
    og_scr = nc.dram_tensor("og_scr", [NM * 128, RVW], BF16, kind="Internal").ap()

    WXt = nc.alloc_sbuf_tensor("WX", [128, 8 * 6144], BF16)
    ARENA = 107 * 1024
    ARt = nc.alloc_sbuf_tensor("AR", [128, ARENA // 2], BF16)
    FIXt = nc.alloc_sbuf_tensor("FIX", [128, 2304], BF16)

    def _view(base, off, shape, dt):
        nb = int(np.prod(shape[1:])) * (4 if dt == F32 else 2)
        ap = base[:, off // 2:(off + nb) // 2]
        if dt == F32:
            ap = ap.bitcast(F32)
        if len(shape) == 3:
            ap = ap.rearrange("p (a n) -> p a n", a=shape[1])
        elif len(shape) == 4:
            ap = ap.rearrange("p (a b n) -> p a b n", a=shape[1], b=shape[2])
        return ap, nb

    class T:
        def __init__(self, name, shape, dt):
            self.name, self.shape, self.dt = name, list(shape), dt
            self.t = None
            self.b = None
            self.dsem = None
            self.dcnt = 0

    def place(base, cap, spec, off0=0):
        off = off0
        for ent in spec:
            grp = ent if isinstance(ent, (tuple, list)) else (ent,)
            buf = Buf(grp[0].name)
            mx = 0
            for t_ in grp:
                t_.t, nb = _view(base, off, t_.shape, t_.dt)
                t_.b = buf
                mx = max(mx, nb)
            off += (mx + 63) // 64 * 64
        assert off <= cap, ("arena overflow", off, cap)
        return off

    WX = T("WX", [128, 8 * 6144], BF16)
    WX.t = WXt[:, :]
    WX.b = Buf("WX")
    A0_IN, A0_OUT, A0_GATE, A0_PLE = 0, 8 * 2560, 8 * 2560 + 8 * 1024, 8 * 2560 + 16 * 1024
    R2_OUT, R2_GATE, R2_PLE = 0, 16 * 1024, 24 * 1024
    wxb = [Buf("wx%d" % i) for i in range(8)]
    wob = [Buf("wo%d" % i) for i in range(8)]
    wgb = [Buf("wg%d" % i) for i in range(8)]
    wpb = [Buf("wp%d" % i) for i in range(8)]

    ident = T("ident", [128, 128], BF16)
    masks = T("masks", [128, 3, 256], BF16)
    snew = T("snew", [128, 256], BF16)
    cf = T("cf", [128, 16 + 256 + NBS * 4 + 32], F32)
    esink = T("esink", [128, NH], F32)
    ss = T("ss", [128, 4], F32)
    ssp = T("ssp", [128, 4], F32)
    bst = T("bst", [128, 4, 8], F32)
    den = T("den", [128, 2 * NH], F32)
    place(FIXt, 4608, [ident, masks, snew, cf, esink, ss, ssp, bst, den])

    xin = [T("xin%d" % i, [128, D], F32) for i in range(3)]
    pin = [T("pin%d" % i, [128, PLE], F32) for i in range(3)]
    rope_t = [T("rope%d" % i, [128, 2, 128], F32) for i in range(3)]
    gpost = T("gpost", [128, D], F32)
    gpre = T("gpre", [128, D], F32)
    smk8 = T("smk8", [128, 16], BF16)
    junk = T("junk", [128, D], BF16)
    junk2 = T("junk2", [128, D], BF16)
    qTd = [T("qTd%d" % i, [128, 8, 128], BF16) for i in range(2)]
    sgA = [T("sgA%d" % i, [128, D], BF16) for i in range(2)]
    xn = T("xn", [128, D], BF16)
    hT = T("hT", [128, 8, 128], BF16)
    qk_f = T("qk_f", [128, 1280], F32)
    rtt = T("rtt", [128, 4, 512], F32)
    q_r = T("q_r", [128, D], BF16)
    k_r = T("k_r", [128, D], BF16)
    k_r2 = T("k_r2", [128, D], BF16)
    kf = T("kf", [128, 256], F32)
    vf = T("vf", [128, 256], F32)
    k_ext = T("k_ext", [128, 8, 128], BF16)
    kT_ext = [T("kT_ext%d" % i, [128, 8, 128], BF16) for i in range(3)]
    V_aug = [T("V_aug%d" % i, [128, NKV, 65], BF16) for i in range(3)]
    qT = T("qT", [128, 8, 128], BF16)
    kT = T("kT", [128, 8, 128], BF16)
    pT = [T("pT%d" % i, [128, 1024], BF16) for i in range(2)]
    sg = T("sg", [128, 2048], BF16)
    v_b = T("v_b", [128, 2048], BF16)
    on_f = T("on_f", [128, D], F32)
    og = T("og", [128, 2048], BF16)
    ogT = T("ogT", [128, 16, 128], BF16)
    tmp = T("tmp", [128, D], F32)
    xa = T("xa", [128, D], F32)
    xb = T("xb", [128, D], BF16)
    xT = T("xT", [128, 8, 128], BF16)
    sig = T("sig", [128, D], F32)
    pb = T("pb", [128, PLE], BF16)
    pTt = T("pTt", [128, 2, 128], BF16)
    xout = [T("xout%d" % i, [128, D], F32) for i in range(3)]
    AT = T("AT", [128, 4, 128], BF16)
    S_f = T("S_f", [128, RH, 2, DV], F32)
    S_b = T("S_b", [128, RH, 2, DV], BF16)
    ogin = [T("ogin%d" % i, [128, 2048], BF16) for i in range(2)]
    stat = T("stat", [128, 32], F32)
    kc_f = [T("kc_f%d" % i, [128, 256], F32) for i in range(2)]
    vc_f = [T("vc_f%d" % i, [128, 256], F32) for i in range(2)]
    smk = [T("smk%d" % i, [128, 256], BF16) for i in range(2)]
    NSS = 3
    Ss = [T("Ss%d" % i, [128, 2, DV], F32) for i in range(NSS)]
    Sb2 = [T("Sb2_%d" % i, [128, 2, DV], BF16) for i in range(2)]
    o_acc = T("o_acc", [128, 2048], F32)
    qTm = T("qTm", [128, 8, 128], BF16)
    k_rm = T("k_rm", [128, D], BF16)
    qTs = T("qTs", [128, 8, 128], BF16)
    k_rs = T("k_rs", [128, D], BF16)
    v_bs = T("v_bs", [128, 2048], BF16)
    sgs = T("sgs", [128, 2048], BF16)

    class RT:
        def __init__(self, i):
            self.i = i
        @property
        def t(self):
            return rtt.t[:, self.i, :]
        @property
        def b(self):
            return rtt.b
    rt = [RT(i) for i in range(4)]

    def layout_A0():
        used = place(ARt, ARENA, [gpost, gpre, smk8, xin[0], xin[1], pin[0], pin[1], rope_t[0], rope_t[1], (rtt, junk), xn, hT, qk_f,
                           q_r, kf, vf, k_ext, kT_ext[0], kT_ext[1], kT_ext[2], V_aug[0], V_aug[1], V_aug[2],
                           qTd[0], qTd[1], pT[0], pT[1], sgA[0], sgA[1], on_f,
                           (og, junk2), ogT, tmp, xa, xb, xT, sig, pb, pTt, xout[0], xout[1],
                           kc_f[0], kc_f[1], vc_f[0], vc_f[1], smk[0], smk[1]])
        place(WXt, 96 * 1024, [xin[2], pin[2], rope_t[2], xout[2]], off0=76 * 1024)

    def layout_R1():
        place(ARt, ARENA, [S_f, S_b, xin[0], xin[1], rope_t[0], rope_t[1], (rtt, junk), (xn, k_r2, k_rm), (hT, AT), (qk_f, og),
                           q_r, k_r, qT, kT, sg, v_b, stat] +
              ([Ss[i] for i in range(NSS)] + [Sb2[0], Sb2[1], o_acc, qTm, qTs, k_rs, v_bs, sgs] if do_sample else []))

    def layout_R1p():
        off = place(ARt, ARENA, [S_f, S_b, gpre, xin[0], xin[1], rope_t[0], rope_t[1], (rtt, junk)])
        bind, off = make_sets([xn, hT, qk_f, k_r, k_r2, v_b], off)
        return bind

    def make_sets(tiles, off0, base=None, cap=None):
        base = ARt if base is None else base
        saved = []
        off = off0
        for par in range(2):
            ents = []
            for t_ in tiles:
                ap, nb = _view(base, off, t_.shape, t_.dt)
                ents.append((t_, ap, Buf("%s_s%d" % (t_.name, par))))
                off += (nb + 63) // 64 * 64
            saved.append(ents)
        assert off <= (ARENA if cap is None else cap), ("arena overflow (sets)", off)

        def bind(par):
            def f():
                for t_, ap, b_ in saved[par]:
                    t_.t = ap
                    t_.b = b_
            return f
        return bind, off

    def layout_R2():
        off = place(ARt, ARENA, [gpost, xin[0], xin[1], pin[0], pin[1], ogin[0], ogin[1], xout[0], xout[1]])
        bind, off = make_sets([ogT, junk2, tmp, xa, xb, xT, sig, pb, pTt, ssp], off)
        return bind

    PS = nc.alloc_psum_tensor("PS", [128, 8, 512], F32)
    psb = [Buf("ps%d" % i) for i in range(8)]

    def ps_bf(bank):
        return PS[:, bank, :].bitcast(BF16)

    op, pe, dma = S.op, S.pe, S.dma
    V, ACT, POOL = nc.vector, nc.scalar, nc.gpsimd
    PEe = nc.tensor
    DEC_P, DEC_S, CM_P, CM_S, RM8, GCOL = 0, 8, 16, 144, 272, 272 + NBS * 4

    def barrier():
        S.barrier()
        for b_ in Buf.ALL:
            b_.w = None
            b_.r = []

    dma("pool", ident.t[:, :], cst_bf[:, 0:128], writes=[ident.b], sbuf=ident)
    dma("pool", masks.t[:, :, :].rearrange("p a n -> p (a n)"), cst_bf[:, 128:128 + 768], writes=[masks.b], sbuf=masks)
    dma("pool", snew.t[:, :], cst_bf[:, 896:896 + 256], writes=[snew.b], sbuf=snew)
    dma("sp", cf.t[:, :], cst_f, writes=[cf.b], sbuf=cf)
    dma("sp", esink.t[:, :], sinks.rearrange("a n -> (a n)").partition_broadcast(128), writes=[esink.b], sbuf=esink)
    op("act", lambda: ACT.activation(out=esink.t[:, :], in_=esink.t[:, :], func=AF.Exp), reads=[esink.b], writes=[esink.b])

    def load_w(dst_off, src_, K, N, bufs, blocks=None, late_bufs=None):
        def one(kt, c0, bb):
            c1 = min(N, c0 + 2048)
            dma("pool", WX.t[:, dst_off + kt * N + c0: dst_off + kt * N + c1],
                src_[kt * 128:(kt + 1) * 128, c0:c1], writes=[bb[kt % 8]], sbuf=bb[kt % 8], waw=False)
        late = []
        for kt in range(K // 128):
            for c0 in range(0, N, 2048):
                if late_bufs is not None and c0 >= 4096:
                    late.append((kt, c0))
                else:
                    one(kt, c0, bufs)
        for kt, c0 in late:
            one(kt, c0, late_bufs)

    def load_gpost(i):
        dma("sp", gpost.t[:, :], norms[i].partition_broadcast(128), writes=[gpost.b], sbuf=gpost)

    def transpose_to(src, ntile, bank, dst, evac_eng, scale_col0=None):
        src2 = src.t if len(src.shape) == 2 else src.t.rearrange("p a n -> p (a n)")
        for g0 in range(0, ntile, 8):
            n = min(8, ntile - g0)
            bk = bank + g0 // 8
            pe([lambda t=t, bk=bk, g0=g0: PEe.transpose(out=ps_bf(bk)[:, (t - g0) * 128:(t - g0 + 1) * 128],
                                                       in_=src2[:, t * 128:(t + 1) * 128], identity=ident.t[:, :])
                for t in range(g0, g0 + n)],
               reads=[src.b, ident.b], writes=[psb[bk]])
            if scale_col0 is not None:
                for t in range(g0, g0 + n):
                    op("act", lambda t=t, bk=bk, g0=g0: ACT.activation(
                        out=dst.t[:, t, :], in_=ps_bf(bk)[:, (t - g0) * 128:(t - g0 + 1) * 128], func=AF.Copy,
                        scale=cf.t[:, scale_col0 + t:scale_col0 + t + 1]),
                       reads=[psb[bk], cf.b], writes=[dst.b])
                continue
            dst_ap = dst.t[:, g0:g0 + n, :].rearrange("p a n -> p (a n)")
            if evac_eng == "act":
                op("act", lambda bk=bk, n=n, dst_ap=dst_ap: ACT.copy(out=dst_ap, in_=ps_bf(bk)[:, 0:n * 128]),
                   reads=[psb[bk]], writes=[dst.b])
            else:
                op("dve", lambda bk=bk, n=n, dst_ap=dst_ap: V.tensor_copy(out=dst_ap, in_=ps_bf(bk)[:, 0:n * 128]),
                   reads=[psb[bk]], writes=[dst.b])

    USE_GPRE = [False]

    def load_gpre(i):
        dma("sp", gpre.t[:, :], norms[i].partition_broadcast(128), writes=[gpre.b], sbuf=gpre)

    def prenorm_and_hT(xt, gi, tbank):
        op("act", lambda: ACT.activation(out=junk.t[:, :], in_=xt.t[:, :], func=AF.Square, accum_out=ss.t[:, 0:1]),
           reads=[xt.b], writes=[junk.b, ss.b])
        op("dve", lambda: V.tensor_scalar(out=ss.t[:, 1:2], in0=ss.t[:, 0:1], scalar1=1.0 / D, scalar2=EPS,
                                          op0=ALU.mult, op1=ALU.add), reads=[ss.b], writes=[ss.b])
        op("act", lambda: ACT.activation(out=ss.t[:, 1:2], in_=ss.t[:, 1:2], func=AF.Sqrt), reads=[ss.b], writes=[ss.b])
        op("dve", lambda: V.reciprocal(out=ss.t[:, 1:2], in_=ss.t[:, 1:2]), reads=[ss.b], writes=[ss.b])
        if USE_GPRE[0]:
            op("dve", lambda: V.tensor_tensor(out=xn.t[:, :], in0=xt.t[:, :], in1=gpre.t[:, :], op=ALU.mult),
               reads=[xt.b, gpre.b], writes=[xn.b], n=1024)
            transpose_to(xn, 8, tbank, hT, "act")
        else:
            op("dve", lambda: V.tensor_copy(out=xn.t[:, :], in_=xt.t[:, :]), reads=[xt.b], writes=[xn.b], n=1024)
            transpose_to(xn, 8, tbank, hT, "act", scale_col0=GCOL + 8 * gi)

    SPLIT = [False]

    def proj(lhs, nkt, woff, wn, c0, ncols, bank0, wb=None):
        nb = (ncols + 511) // 512
        wb = wxb if wb is None else wb
        groups = []
        for kt in range(nkt):
            fns = []
            for j in range(nb):
                cc0 = c0 + j * 512
                cc1 = min(c0 + ncols, cc0 + 512)
                fns.append(lambda kt=kt, j=j, cc0=cc0, cc1=cc1: PEe.matmul(
                    PS[:, bank0 + j, 0:cc1 - cc0], lhsT=lhs.t[:, kt, :],
                    rhs=WX.t[:, woff + kt * wn + cc0: woff + kt * wn + cc1],
                    start=(kt == 0), stop=(kt == nkt - 1)))
            groups.append((kt, fns))
        wr = [psb[bank0 + j] for j in range(nb)]
        if SPLIT[0]:
            for kt, fns in groups:
                pe(fns, reads=[lhs.b, wb[kt % 8]], writes=wr, ncols=512)
        else:
            allf = [f for _, fns in groups for f in fns]
            pe(allf, reads=[lhs.b] + [wb[k % 8] for k in range(min(nkt, 8))], writes=wr, ncols=512)

    def resid_tail(xt, ybank, pt, woff_gate, woff_ple, out_ap, is_out, oslot, tbx=None, tbp=None, gb=None, pleb=None):
        tbx = ybank + 2 if tbx is None else tbx
        tbp = ybank + 3 if tbp is None else tbp
        gb = ybank + 4 if gb is None else gb
        pleb = ybank if pleb is None else pleb
        ss = ssp
        yb = [psb[ybank], psb[ybank + 1]]
        yap = PS[:, ybank:ybank + 2, :].rearrange("p a n -> p (a n)")
        op("act", lambda: ACT.activation(out=junk2.t[:, :], in_=yap, func=AF.Square, accum_out=ss.t[:, 2:3]),
           reads=yb, writes=[junk2.b, ss.b])
        op("dve", lambda: V.tensor_scalar(out=ss.t[:, 3:4], in0=ss.t[:, 2:3], scalar1=1.0 / D, scalar2=EPS,
                                          op0=ALU.mult, op1=ALU.add), reads=[ss.b], writes=[ss.b])
        op("act", lambda: ACT.activation(out=ss.t[:, 3:4], in_=ss.t[:, 3:4], func=AF.Sqrt), reads=[ss.b], writes=[ss.b])
        op("dve", lambda: V.reciprocal(out=ss.t[:, 3:4], in_=ss.t[:, 3:4]), reads=[ss.b], writes=[ss.b])
        op("dve", lambda: V.scalar_tensor_tensor(out=tmp.t[:, :], in0=yap, scalar=ss.t[:, 3:4],
                                                 in1=gpost.t[:, :], op0=ALU.mult, op1=ALU.mult),
           reads=yb + [ss.b, gpost.b], writes=[tmp.b])
        op("dve", lambda: V.tensor_tensor(out=xa.t[:, :], in0=xt.t[:, :], in1=tmp.t[:, :], op=ALU.add),
           reads=[xt.b, tmp.b], writes=[xa.b], n=1024)
        op("act", lambda: ACT.copy(out=xb.t[:, :], in_=xa.t[:, :]), reads=[xa.b], writes=[xb.b])
        op("act", lambda: ACT.copy(out=pb.t[:, :], in_=pt.t[:, :]), reads=[pt.b], writes=[pb.b])
        transpose_to(xb, 8, tbx, xT, "dve")
        transpose_to(pb, 2, tbp, pTt, "dve")
        proj(xT, 8, woff_gate, D, 0, D, gb, wb=wgb)
        op("act", lambda: ACT.activation(out=sig.t[:, :], in_=PS[:, gb:gb + 2, :].rearrange("p a n -> p (a n)"),
                                         func=AF.Tanh, scale=0.5), reads=[psb[gb], psb[gb + 1]], writes=[sig.b], n=1024)
        proj(pTt, 2, woff_ple, D, 0, D, pleb, wb=wpb)
        pb_ = [psb[pleb], psb[pleb + 1]]
        pap = PS[:, pleb:pleb + 2, :].rearrange("p a n -> p (a n)")
        op("dve", lambda: V.scalar_tensor_tensor(out=tmp.t[:, :], in0=sig.t[:, :], scalar=1.0, in1=pap,
                                                 op0=ALU.add, op1=ALU.mult),
           reads=pb_ + [sig.b], writes=[tmp.b], n=1024)
        xo = xout[oslot]
        op("dve", lambda: V.scalar_tensor_tensor(out=xo.t[:, :], in0=tmp.t[:, :], scalar=0.5, in1=xa.t[:, :],
                                                 op0=ALU.mult, op1=ALU.add),
           reads=[xa.b, tmp.b], writes=[xo.b], n=1024)
        dma("sp", out_ap, xo.t[:, :], reads=[xo.b], sbuf=xo, is_out=is_out)

    def do_rope(srcT, src_ap, dstT, dst_ap, nh, half, cst, eng):
        E_ = V if eng == "dve" else POOL
        sv = src_ap.rearrange("p (h t f) -> p h t f", h=nh, t=2)
        dv = dst_ap.rearrange("p (h t f) -> p h t f", h=nh, t=2)
        x1, x2 = sv[:, :, 0, :], sv[:, :, 1, :]
        cos = cst.t[:, 0:1, 0:half].to_broadcast([128, nh, half])
        sin = cst.t[:, 1:2, 0:half].to_broadcast([128, nh, half])
        n = nh * half

        def v3(i):
            return rtt.t[:, i, 0:n].rearrange("p (h f) -> p h f", h=nh)
        rd = [srcT.b, cst.b]
        op(eng, lambda: E_.tensor_tensor(out=v3(0), in0=x1, in1=cos, op=ALU.mult), reads=rd, writes=[rtt.b])
        op(eng, lambda: E_.tensor_tensor(out=v3(1), in0=x2, in1=sin, op=ALU.mult), reads=rd, writes=[rtt.b])
        op(eng, lambda: E_.tensor_tensor(out=v3(2), in0=x2, in1=cos, op=ALU.mult), reads=rd, writes=[rtt.b])
        op(eng, lambda: E_.tensor_tensor(out=v3(3), in0=x1, in1=sin, op=ALU.mult), reads=rd, writes=[rtt.b])
        op(eng, lambda: E_.tensor_tensor(out=dv[:, :, 0, :], in0=v3(0), in1=v3(1), op=ALU.subtract),
           reads=[rtt.b], writes=[dstT.b])
        op(eng, lambda: E_.tensor_tensor(out=dv[:, :, 1, :], in0=v3(2), in1=v3(3), op=ALU.add),
           reads=[rtt.b], writes=[dstT.b])

    def a0_load(ci):
        s = ci % 3
        dma("sp", xin[s].t[:, :], x_all[ci * 128:(ci + 1) * 128, :], writes=[xin[s].b], sbuf=xin[s])
        dma("sp", pin[s].t[:, :], p0_all[ci * 128:(ci + 1) * 128, :], writes=[pin[s].b], sbuf=pin[s])
        dma("sp", rope_t[s].t[:, :, 0:32], ropeA[ci], writes=[rope_t[s].b], sbuf=rope_t[s])

    def kext_fill(src_tile):
        kfv = src_tile.t[:, :].rearrange("p (k d) -> p k d", k=NKV)
        ke = k_ext.t[:, :, :].rearrange("p (k v) n -> p k v n", v=2)
        op("pool", lambda: POOL.tensor_copy(out=ke[:, :, 0, 0:64], in_=kfv), reads=[src_tile.b], writes=[k_ext.b])
        op("pool", lambda: POOL.tensor_copy(out=ke[:, :, 1, 64:128], in_=kfv), reads=[src_tile.b], writes=[k_ext.b])

    def a0_chunk(ci):
        s = ci % 3
        is_sample = (ci == NA - 1)
        xt, pt, cst = xin[s], pin[s], rope_t[s]
        cur, prv = ci % 3, (ci + 2) % 3
        qT_, sg_ = qTd[ci % 2], sgA[ci % 2]
        prenorm_and_hT(xt, 0, 0)
        rstd = ss.t[:, 1:2]
        proj(hT, 8, A0_IN, 2560, 1024, 512, 1)
        op("act", lambda: ACT.activation(out=qk_f.t[:, D:D + 256], in_=PS[:, 1, 0:256], func=AF.Copy, scale=rstd),
           reads=[psb[1], ss.b], writes=[qk_f.b], n=256)
        op("act", lambda: ACT.activation(out=vf.t[:, :], in_=PS[:, 1, 256:512], func=AF.Copy, scale=rstd),
           reads=[psb[1], ss.b], writes=[vf.b], n=256)
        for r_ in range(2):
            proj(hT, 8, A0_IN, 2560, 512 * r_, 512, r_)
            op("act", lambda r_=r_: ACT.activation(out=qk_f.t[:, 512 * r_:512 * (r_ + 1)], in_=PS[:, r_, :],
                                                   func=AF.Copy, scale=rstd), reads=[psb[r_], ss.b], writes=[qk_f.b], n=512)
        for r_ in range(2):
            proj(hT, 8, A0_IN, 2560, 1536 + 512 * r_, 512, r_)
            op("act", lambda r_=r_: ACT.activation(out=sg_.t[:, 512 * r_:512 * (r_ + 1)], in_=PS[:, r_, :],
                                                   func=AF.Silu, scale=rstd), reads=[psb[r_], ss.b], writes=[sg_.b], n=512)
        do_rope(qk_f, qk_f.t[:, 0:D], q_r, q_r.t[:, :], NH, 32, cst, "dve")
        do_rope(qk_f, qk_f.t[:, D:D + 256], kf, kf.t[:, :], NKV, 32, cst, "dve")
        kext_fill(kf)
        op("pool", lambda: POOL.tensor_copy(out=V_aug[cur].t[:, :, 0:64], in_=vf.t[:, :].rearrange("p (k d) -> p k d", k=NKV)),
           reads=[vf.b], writes=[V_aug[cur].b])
        if ci == 2 * NCH - 1:
            dma("sp", kwin_p, kf.t[:, :], reads=[kf.b], sbuf=kf, is_out=True)
            dma("sp", vwin_p, vf.t[:, :], reads=[vf.b], sbuf=vf, is_out=True)
        if is_sample:
            wh = [Buf("wst%d" % i) for i in range(8)]
            for b in range(NBS):
                dma("sp", kwin_s[b, 120:128, :], kf.t[b * LS:(b + 1) * LS, :], reads=[kf.b], sbuf=wh[b % 4], is_out=True)
                dma("sp", vwin_s[b, 120:128, :], vf.t[b * LS:(b + 1) * LS, :], reads=[vf.b], sbuf=wh[4 + b % 4], is_out=True)
        transpose_to(q_r, 8, 0, qT_, "act")
        transpose_to(k_ext, 8, 1, kT_ext[cur], "dve")
        if not is_sample:
            has_prev = not (ci == 0)
            mprev = 2 if ci == NCH else 1
            blocks = ([(kT_ext[prv], V_aug[prv], masks.t[:, mprev, :], masks.b)] if has_prev else []) + \
                     [(kT_ext[cur], V_aug[cur], masks.t[:, 0, :], masks.b)]
            nb_ = len(blocks)
            for kvh in range(NKV):
                for bi, (kt_, va_, mk, mkb) in enumerate(blocks):
                    attend_block(kvh, qT_, kt_, va_, mk, mkb, 2 + bi, pT[kvh % 2], bi * 512, bi == 0, bi == nb_ - 1)
                normalize(kvh)
        else:
            a0_attend_sample(cur, qT_)
        op("dve", lambda: V.tensor_tensor(out=og.t[:, 0:D], in0=on_f.t[:, :], in1=sg_.t[:, :], op=ALU.mult),
           reads=[on_f.b, sg_.b], writes=[og.b], n=1024)
        transpose_to(og, 8, 5, ogT, "act")
        proj(ogT, 8, A0_OUT, D, 0, D, 6, wb=wob)
        resid_tail(xt, 6, pt, A0_GATE, A0_PLE, x1_scr[ci * 128:(ci + 1) * 128, :], False, s, tbx=5, tbp=5, gb=6, pleb=6)

    OB = [lambda kvh: 4]

    def normalize(kvh):
        ob = OB[0](kvh)
        ov = PS[:, ob, 0:260].rearrange("p (g d) -> p g d", g=4)
        op("dve", lambda: V.tensor_tensor(out=den.t[:, 4 * kvh:4 * kvh + 4], in0=ov[:, :, 64],
                                          in1=esink.t[:, 4 * kvh:4 * kvh + 4], op=ALU.add),
           reads=[psb[ob], esink.b], writes=[den.b], n=4)
        op("dve", lambda: V.reciprocal(out=den.t[:, NH + 4 * kvh:NH + 4 * kvh + 4], in_=den.t[:, 4 * kvh:4 * kvh + 4]),
           reads=[den.b], writes=[den.b], n=4)
        op("dve", lambda: V.tensor_tensor(
            out=on_f.t[:, kvh * 256:(kvh + 1) * 256].rearrange("p (g d) -> p g d", g=4), in0=ov[:, :, 0:64],
            in1=den.t[:, NH + 4 * kvh:NH + 4 * kvh + 4].unsqueeze(2).to_broadcast([128, 4, 64]), op=ALU.mult),
           reads=[psb[ob], den.b], writes=[on_f.b], n=256)

    def attend_block(kvh, qT_, kt_, va_, mk, mkb, sbank, ptile, pcol, first, last):
        ob = OB[0](kvh)
        fns = []
        for var in range(2):
            o_ = PS[:, sbank, var * 256:(var + 1) * 256]
            fns.append(lambda o_=o_, var=var: PEe.matmul(
                o_, lhsT=kt_.t[:, kvh * 2 + var, :],
                rhs=qT_.t[:, 2 * kvh:2 * kvh + 2, :].rearrange("p a n -> p (a n)"), start=True, stop=False))
            fns.append(lambda o_=o_: PEe.matmul(o_, lhsT=ident.t[:, :], rhs=mk, start=False, stop=True))
        pe(fns, reads=[kt_.b, qT_.b, ident.b, mkb], writes=[psb[sbank]], ncols=256)
        op("act", lambda: ACT.activation(out=ptile.t[:, pcol:pcol + 512], in_=PS[:, sbank, :], func=AF.Exp, scale=0.125),
           reads=[psb[sbank]], writes=[ptile.b], n=512)
        fns = []
        for var in range(2):
            for tl in range(2):
                g = 2 * tl + var
                c0 = pcol + var * 256 + tl * 128
                fns.append(lambda g=g, c0=c0: PEe.matmul(
                    PS[:, ob, g * 65:(g + 1) * 65], lhsT=ptile.t[:, c0:c0 + 128],
                    rhs=va_.t[:, kvh, :], start=(first and g == 0), stop=last, skip_group_check=True))
        pe(fns, reads=[ptile.b, va_.b], writes=[psb[ob]], ncols=65)

    def a0_attend_sample(cur, qT_):
        prv = (cur + 1) % 3
        OB[0] = lambda kvh: 4 + kvh
        for kvh in range(NKV):
            attend_block(kvh, qT_, kT_ext[cur], V_aug[cur], snew.t[:, :], snew.b, 2 + kvh % 2, pT[kvh % 2], 0, True, False)
        for i in range(2):
            op("dve", lambda i=i: V.memset(pT[i].t[:, 0:512], 0.0), writes=[pT[i].b], n=512)
        dma("pool", smk8.t[:, 0:8], cst_smask[0][:, 0:8], writes=[smk8.b], sbuf=smk8)
        dma("pool", smk8.t[:, 8:16], cst_smask[0][:, 0:8], writes=[smk8.b], sbuf=smk8)
        for b in range(NBS):
            sl = b % 2
            cols = slice(b * LS, (b + 1) * LS)
            dma("sp", kc_f[sl].t[:, :], cache_k[b], writes=[kc_f[sl].b], sbuf=kc_f[sl])
            dma("sp", vc_f[sl].t[:, :], cache_v[b], writes=[vc_f[sl].b], sbuf=vc_f[sl])
            kcv = kc_f[sl].t[:, :].rearrange("p (k d) -> p k d", k=NKV)
            ke = k_ext.t[:, :, :].rearrange("p (k v) n -> p k v n", v=2)
            op("dve", lambda kcv=kcv, ke=ke: V.tensor_copy(out=ke[:, :, 0, 0:64], in_=kcv), reads=[kc_f[sl].b], writes=[k_ext.b], n=256)
            op("dve", lambda kcv=kcv, ke=ke: V.tensor_copy(out=ke[:, :, 1, 64:128], in_=kcv), reads=[kc_f[sl].b], writes=[k_ext.b], n=256)
            op("act", lambda sl=sl: ACT.copy(out=V_aug[prv].t[:, :, 0:64],
                                             in_=vc_f[sl].t[:, :].rearrange("p (k d) -> p k d", k=NKV)),
               reads=[vc_f[sl].b], writes=[V_aug[prv].b], n=256)
            transpose_to(k_ext, 8, b % 2, kT_ext[prv], "dve")
            for kvh in range(NKV):
                sbank = 2 + kvh % 2
                ptile = pT[kvh % 2]
                ob = 4 + kvh
                fns = []
                for var in range(2):
                    o_ = PS[:, sbank, var * 16:(var + 1) * 16]
                    fns.append(lambda o_=o_, var=var, kvh=kvh, cols=cols: PEe.matmul(
                        o_, lhsT=kT_ext[prv].t[:, kvh * 2 + var, :], rhs=qT_.t[:, 2 * kvh:2 * kvh + 2, cols],
                        start=True, stop=False))
                    fns.append(lambda o_=o_: PEe.matmul(o_, lhsT=ident.t[:, :], rhs=smk8.t[:, :], start=False, stop=True))
                pe(fns, reads=[kT_ext[prv].b, qT_.b, ident.b, smk8.b], writes=[psb[sbank]], ncols=16)
                pz = ptile.t[:, 0:512].rearrange("p (a n) -> p a n", a=4)
                op("act", lambda sbank=sbank, pz=pz, cols=cols: ACT.activation(
                    out=pz[:, :, cols], in_=PS[:, sbank, 0:32].rearrange("p (a l) -> p a l", a=4), func=AF.Exp, scale=0.125),
                   reads=[psb[sbank]], writes=[ptile.b], n=32)
                fns = []
                for var in range(2):
                    for tl in range(2):
                        g = 2 * tl + var
                        c0 = var * 256 + tl * 128
                        fns.append(lambda g=g, c0=c0, kvh=kvh, ptile=ptile, ob=ob, b=b: PEe.matmul(
                            PS[:, ob, g * 65:(g + 1) * 65], lhsT=ptile.t[:, c0:c0 + 128], rhs=V_aug[prv].t[:, kvh, :],
                            start=False, stop=(b == NBS - 1), skip_group_check=True))
                pe(fns, reads=[ptile.b, V_aug[prv].b], writes=[psb[ob]], ncols=65)
                op("dve", lambda pz=pz, cols=cols: V.memset(pz[:, :, cols], 0.0), writes=[ptile.b], n=32)
        for kvh in range(NKV):
            normalize(kvh)

    def r1_slot(ci):
        return (ci % 2) if ci != NA - 1 else 1 - (NCH % 2)

    def r1_load(ci):
        s = r1_slot(ci)
        dma("sp", xin[s].t[:, :], x1_scr[ci * 128:(ci + 1) * 128, :], writes=[xin[s].b], sbuf=xin[s])
        dma("sp", rope_t[s].t[:, :, :], ropeR[ci], writes=[rope_t[s].b], sbuf=rope_t[s])

    G128 = [float((1.0 - 2.0 ** (-5.0 - h)) ** 128) for h in range(RH)]
    G8 = [float((1.0 - 2.0 ** (-5.0 - h)) ** 8) for h in range(RH)]

    def r1_front(ci, kv_only, is_sample, o_kr, o_qT, o_vb, o_sg):
        s = r1_slot(ci)
        xt, cst = xin[s], rope_t[s]
        prenorm_and_hT(xt, 1, 7)
        dec0 = DEC_S if is_sample else DEC_P
        op("dve", lambda: V.tensor_scalar(out=bst.t[:, 0, :], in0=cf.t[:, dec0:dec0 + 8], scalar1=ss.t[:, 1:2],
                                          scalar2=None, op0=ALU.mult), reads=[cf.b, ss.b], writes=[bst.b])
        if not kv_only:
            proj(hT, 8, 0, 6144, 0, 2048, 0)
        else:
            proj(hT, 8, 0, 6144, 1024, 1024, 2)
        proj(hT, 8, 0, 6144, 2048, 2048, 4)
        if not kv_only:
            for j in range(4):
                op("act", lambda j=j: ACT.activation(out=qk_f.t[:, j * 256:(j + 1) * 256],
                                                     in_=PS[:, j // 2, (j % 2) * 256:(j % 2 + 1) * 256],
                                                     func=AF.Copy, scale=bst.t[:, 0, j:j + 1]),
                   reads=[psb[j // 2], bst.b], writes=[qk_f.b])
            do_rope(qk_f, qk_f.t[:, 0:D], q_r, q_r.t[:, :], RH, 128, cst, "dve")
        for j in range(4, 8):
            op("act", lambda j=j: ACT.activation(out=qk_f.t[:, (j - 4) * 256:(j - 3) * 256],
                                                 in_=PS[:, j // 2, (j % 2) * 256:(j % 2 + 1) * 256],
                                                 func=AF.Copy, scale=bst.t[:, 0, j:j + 1]),
               reads=[psb[j // 2], bst.b], writes=[qk_f.b])
        do_rope(qk_f, qk_f.t[:, 0:D], o_kr, o_kr.t[:, :], RH, 128, cst, "dve")
        op("act", lambda: ACT.activation(out=o_vb.t[:, :], in_=PS[:, 4:8, :].rearrange("p a n -> p (a n)"),
                                         func=AF.Copy, scale=ss.t[:, 1:2]),
           reads=[psb[4], psb[5], psb[6], psb[7], ss.b], writes=[o_vb.b])
        if not kv_only:
            proj(hT, 8, 0, 6144, 4096, 2048, 0, wb=wgb)
            op("act", lambda: ACT.activation(out=o_sg.t[:, :], in_=PS[:, 0:4, :].rearrange("p a n -> p (a n)"),
                                             func=AF.Silu, scale=ss.t[:, 1:2]),
               reads=[psb[0], psb[1], psb[2], psb[3], ss.b], writes=[o_sg.b])
            transpose_to(q_r, 8, 4, o_qT, "act")
            transpose_to(o_kr, 8, 5, kT, "dve")

    def r1_AT(cm, qTt):
        fns = []
        for h in range(RH):
            for j in range(2):
                fns.append(lambda h=h, j=j: PEe.matmul(PS[:, 6, h * 128:(h + 1) * 128], lhsT=kT.t[:, 2 * h + j, :],
                                                       rhs=qTt.t[:, 2 * h + j, :], start=(j == 0), stop=(j == 1)))
        pe(fns, reads=[kT.b, qTt.b], writes=[psb[6]])
        op("dve", lambda: V.tensor_tensor(out=AT.t[:, :, :], in0=PS[:, 6, :].rearrange("p (h n) -> p h n", h=4),
                                          in1=cf.t[:, cm:cm + 128].unsqueeze(1).to_broadcast([128, 4, 128]),
                                          op=ALU.mult), reads=[psb[6], cf.b], writes=[AT.b])

    def r1_chunk(ci, kv_only):
        r1_front(ci, kv_only, False, k_r, qT, v_b, sg)
        if not kv_only:
            r1_AT(CM_P, qT)
            for h in range(RH):
                fns = [lambda h=h: PEe.matmul(PS[:, h, :], lhsT=AT.t[:, h, :], rhs=v_b.t[:, h * 512:(h + 1) * 512],
                                              start=True, stop=False)]
                for j in range(2):
                    fns.append(lambda h=h, j=j: PEe.matmul(PS[:, h, :], lhsT=qT.t[:, 2 * h + j, :], rhs=S_b.t[:, h, j, :],
                                                           start=False, stop=(j == 1)))
                pe(fns, reads=[AT.b, v_b.b, qT.b, S_b.b], writes=[psb[h]], ncols=512)
        for h in range(RH):
            op("dve", lambda h=h: V.tensor_scalar(out=k_r2.t[:, h * 256:(h + 1) * 256], in0=k_r.t[:, h * 256:(h + 1) * 256],
                                                  scalar1=G128[h], scalar2=None, op0=ALU.mult),
               reads=[k_r.b], writes=[k_r2.b], n=256)
        for h in range(RH):
            for j in range(2):
                bk = (j if kv_only else 4 + (2 * h + j) % 4)
                pe([lambda h=h, j=j, bk=bk: PEe.matmul(PS[:, bk, :], lhsT=k_r2.t[:, (2 * h + j) * 128:(2 * h + j + 1) * 128],
                                                       rhs=v_b.t[:, h * 512:(h + 1) * 512], start=True, stop=True)],
                   reads=[k_r2.b, v_b.b], writes=[psb[bk]], ncols=512)
                op("dve", lambda h=h, j=j, bk=bk: V.scalar_tensor_tensor(
                    out=S_f.t[:, h, j, :], in0=S_f.t[:, h, j, :], scalar=G128[h], in1=PS[:, bk, :],
                    op0=ALU.mult, op1=ALU.add), reads=[psb[bk], S_f.b, S_b.b], writes=[S_f.b])
        if (not kv_only) or ci == NCH - 1:
            op("act", lambda: ACT.copy(out=S_b.t[:, :, :, :].rearrange("p h j n -> p (h j n)"),
                                       in_=S_f.t[:, :, :, :].rearrange("p h j n -> p (h j n)")),
               reads=[S_f.b], writes=[S_b.b], n=4096)
        if not kv_only:
            r1_finish(ci, [psb[0], psb[1], psb[2], psb[3]], PS[:, 0:4, :], sg)

    def r1_finish(ci, obufs, oap, sgt):
        rb = (lambda h: [obufs[h]]) if obufs else (lambda h: [o_acc.b])
        for h in range(RH):
            op("dve", lambda h=h: V.bn_stats(out=stat.t[:, h * 8:h * 8 + 6], in_=oap[:, h, :]), reads=rb(h), writes=[stat.b])
            op("dve", lambda h=h: V.bn_aggr(out=bst.t[:, 2, 2 * h:2 * h + 2], in_=stat.t[:, h * 8:h * 8 + 6]),
               reads=[stat.b], writes=[bst.b])
        mv = bst.t[:, 2, :].rearrange("p (h t) -> p h t", t=2)
        op("dve", lambda: V.tensor_scalar(out=bst.t[:, 3, 0:4], in0=mv[:, :, 1], scalar1=EPS, scalar2=None,
                                          op0=ALU.add), reads=[bst.b], writes=[bst.b])
        op("act", lambda: ACT.activation(out=bst.t[:, 3, 0:4], in_=bst.t[:, 3, 0:4], func=AF.Sqrt),
           reads=[bst.b], writes=[bst.b])
        op("dve", lambda: V.reciprocal(out=bst.t[:, 3, 0:4], in_=bst.t[:, 3, 0:4]), reads=[bst.b], writes=[bst.b])
        op("dve", lambda: V.scalar_tensor_tensor(out=bst.t[:, 3, 4:8], in0=mv[:, :, 0], scalar=-1.0, in1=bst.t[:, 3, 0:4],
                                                 op0=ALU.mult, op1=ALU.mult), reads=[bst.b], writes=[bst.b])
        for h in range(RH):
            op("act", lambda h=h: ACT.activation(out=og.t[:, h * 512:(h + 1) * 512], in_=oap[:, h, :], func=AF.Identity,
                                                 scale=bst.t[:, 3, h:h + 1], bias=bst.t[:, 3, 4 + h:5 + h]),
               reads=rb(h) + [bst.b], writes=[og.b])
        op("dve", lambda: V.tensor_tensor(out=og.t[:, :], in0=og.t[:, :], in1=sgt.t[:, :], op=ALU.mult),
           reads=[og.b, sgt.b], writes=[og.b], n=2048)
        mi = ci - NCH
        dma("sp", og_scr[mi * 128:(mi + 1) * 128, :], og.t[:, :], reads=[og.b], sbuf=og)

    def r1_sample_front(ci):
        r1_front(ci, False, True, k_rs, qTs, v_bs, sgs)
        r1_AT(CM_S, qTs)
        for h in range(RH):
            pe([lambda h=h: PEe.matmul(PS[:, h, :], lhsT=AT.t[:, h, :], rhs=v_bs.t[:, h * 512:(h + 1) * 512],
                                       start=True, stop=True)], reads=[AT.b, v_bs.b], writes=[psb[h]])
        op("act", lambda: ACT.copy(out=o_acc.t[:, :], in_=PS[:, 0:4, :].rearrange("p a n -> p (a n)")),
           reads=[psb[0], psb[1], psb[2], psb[3]], writes=[o_acc.b])

    def state_load(idx):
        b, h = idx // RH, idx % RH
        St = Ss[idx % NSS]
        dma("sp", St.t[:, :, :], state_in[b, h].rearrange("(j p) e -> p j e", p=128), writes=[St.b], sbuf=St)

    def r1_sample_b(b):
        cols = slice(b * LS, (b + 1) * LS)
        op("pool", lambda: POOL.tensor_copy(out=qTm.t[:, :, cols], in_=qTs.t[:, :, cols]), reads=[qTs.b], writes=[qTm.b])
        op("dve", lambda: V.tensor_tensor(out=k_rm.t[:, :].rearrange("p (h n) -> p h n", h=4),
                                              in0=k_rs.t[:, :].rearrange("p (h n) -> p h n", h=4),
                                              in1=cf.t[:, RM8 + 4 * b:RM8 + 4 * b + 4].unsqueeze(2).to_broadcast([128, 4, 256]),
                                              op=ALU.mult), reads=[k_rs.b, cf.b], writes=[k_rm.b])
        for h in range(RH):
            idx = b * RH + h
            if idx + 2 < NBS * RH:
                state_load(idx + 2)
            St, Sb_ = Ss[idx % NSS], Sb2[idx % 2]
            op("act", lambda St=St, Sb_=Sb_: ACT.copy(out=Sb_.t[:, :, :].rearrange("p j n -> p (j n)"),
                                                      in_=St.t[:, :, :].rearrange("p j n -> p (j n)")),
               reads=[St.b], writes=[Sb_.b])
            pe([lambda h=h, j=j, Sb_=Sb_: PEe.matmul(PS[:, h, :], lhsT=qTm.t[:, 2 * h + j, :], rhs=Sb_.t[:, j, :],
                                                    start=(j == 0), stop=(j == 1)) for j in range(2)],
               reads=[qTm.b, Sb_.b], writes=[psb[h]], ncols=512)
            op("dve", lambda h=h: V.tensor_tensor(out=o_acc.t[:, h * 512:(h + 1) * 512], in0=PS[:, h, :],
                                                  in1=o_acc.t[:, h * 512:(h + 1) * 512], op=ALU.add),
               reads=[psb[h], o_acc.b], writes=[o_acc.b])
            for j in range(2):
                bk = 4 + (2 * h + j) % 4
                pe([lambda h=h, j=j, bk=bk: PEe.matmul(PS[:, bk, :], lhsT=k_rm.t[:, (2 * h + j) * 128:(2 * h + j + 1) * 128],
                                                       rhs=v_bs.t[:, h * 512:(h + 1) * 512], start=True, stop=True)],
                   reads=[k_rm.b, v_bs.b], writes=[psb[bk]], ncols=512)
                op("dve", lambda h=h, j=j, bk=bk, St=St: V.scalar_tensor_tensor(
                    out=St.t[:, j, :], in0=St.t[:, j, :], scalar=G8[h], in1=PS[:, bk, :],
                    op0=ALU.mult, op1=ALU.add), reads=[psb[bk], St.b, Sb_.b], writes=[St.b])
            dma("sp", rst_s[b, h].rearrange("(j p) e -> p j e", p=128), St.t[:, :, :], reads=[St.b], sbuf=St, is_out=True)
        op("pool", lambda: POOL.memset(qTm.t[:, :, cols], 0.0), writes=[qTm.b])

    def r2_load(mi):
        s = mi % 2
        ci = NCH + mi
        dma("sp", xin[s].t[:, :], x1_scr[ci * 128:(ci + 1) * 128, :], writes=[xin[s].b], sbuf=xin[s])
        dma("sp", pin[s].t[:, :], p1_all[mi * 128:(mi + 1) * 128, :], writes=[pin[s].b], sbuf=pin[s])
        dma("sp", ogin[s].t[:, :], og_scr[mi * 128:(mi + 1) * 128, :], writes=[ogin[s].b], sbuf=ogin[s])

    def r2_chunk(mi):
        s = mi % 2
        transpose_to(ogin[s], 16, 0, ogT, "act")
        proj(ogT, 16, R2_OUT, D, 0, D, 2, wb=wob)
        out_ap = y_p[mi * 128:(mi + 1) * 128, :] if mi < NCH else y_s
        resid_tail(xin[s], 2, pin[s], R2_GATE, R2_PLE, out_ap, True, s, tbx=4, tbp=5, gb=6, pleb=4)

    layout_A0()
    load_w(A0_IN, w_in_attn, D, 2560, wxb)
    load_w(A0_OUT, w_out_attn, D, D, wob)
    load_w(A0_GATE, w_gate[0], D, D, wgb)
    load_w(A0_PLE, w_ple[0], PLE, D, wpb)
    load_gpost(2)
    load_gpre(0)
    USE_GPRE[0] = True
    for i in range(3):
        op("pool", lambda i=i: POOL.memset(V_aug[i].t[:, :, :], 1.0), writes=[V_aug[i].b])
    op("pool", lambda: POOL.memset(k_ext.t[:, :, :], 0.0), writes=[k_ext.b])
    nA = NA if do_sample else NA - 1
    a0_load(0)
    for ci in range(nA):
        if ci + 1 < nA:
            a0_load(ci + 1)
        SPLIT[0] = (ci == 0)
        a0_chunk(ci)
    SPLIT[0] = False
    if do_sample:
        dma("sp", kwin_s[:, 0:120, :], cache_k[:, 8:128, :], sbuf=Buf("d2d_k"), is_out=True)
        dma("sp", vwin_s[:, 0:120, :], cache_v[:, 8:128, :], sbuf=Buf("d2d_v"), is_out=True)
    barrier()

    bindR1p = layout_R1p()
    load_gpre(1)
    USE_GPRE[0] = True
    load_w(0, w_in_ret, D, 6144, wxb, late_bufs=wgb)
    op("dve", lambda: V.memset(S_f.t[:, :, :, :], 0.0), writes=[S_f.b])
    op("dve", lambda: V.memset(S_b.t[:, :, :, :], 0.0), writes=[S_b.b])
    r1_load(0)
    for ci in range(NCH):
        if ci + 1 < NCH:
            r1_load(ci + 1)
        S.ctx = bindR1p(ci % 2)
        S.ctx()
        SPLIT[0] = (ci == 0)
        r1_chunk(ci, True)
        SPLIT[0] = False
    S.ctx = None
    barrier()
    USE_GPRE[0] = False
    layout_R1()
    if do_sample:
        op("pool", lambda: POOL.memset(qTm.t[:, :, :], 0.0), writes=[qTm.b])
    order = list(range(NCH, 2 * NCH))
    if do_sample:
        sci = NA - 1
        r1_load(sci)
        state_load(0)
        state_load(1)
        r1_load(order[0])
        r1_sample_front(sci)
    else:
        r1_load(order[0])
    bper = (NBS + NCH - 1) // NCH
    nb_done = 0
    for k, ci in enumerate(order):
        if k + 1 < len(order):
            r1_load(order[k + 1])
        r1_chunk(ci, False)
        if do_sample:
            for _ in range(bper):
                if nb_done < NBS:
                    r1_sample_b(nb_done)
                    nb_done += 1
    dma("sp", rst_p.rearrange("h (j p) e -> p h j e", p=128), S_f.t[:, :, :, :], reads=[S_f.b], sbuf=S_f, is_out=True)
    if do_sample:
        r1_finish(NA - 1, None, o_acc.t[:, :].rearrange("p (h n) -> p h n", h=4), sgs)
    barrier()

    bindR2 = layout_R2()
    load_w(R2_OUT, w_out_ret, RVW, D, wob)
    load_w(R2_GATE, w_gate[1], D, D, wgb)
    load_w(R2_PLE, w_ple[1], PLE, D, wpb)
    load_gpost(3)
    nM = NM if do_sample else NM - 1
    r2_load(0)
    for mi in range(nM):
        if mi + 1 < nM:
            r2_load(mi + 1)
        S.ctx = bindR2(mi % 2)
        S.ctx()
        SPLIT[0] = (mi == 0)
        r2_chunk(mi)
        SPLIT[0] = False
    S.ctx = None
    S.finish()
    return nc


def _rope_tab(pos, half):
    inv = (np.float32(THETA) ** (-(np.arange(half, dtype=np.float32)) / np.float32(half))).astype(np.float32)
    ang = pos.astype(np.float32)[:, None] * inv[None, :]
    return np.stack([np.cos(ang), np.sin(ang)], axis=1).astype(np.float32)


def make_consts(NCH, half_idx):
    T = NCH * 128
    NA = 2 * NCH + 1
    pos_pref = (half_idx - 1) * T + np.arange(T)
    pos_main = half_idx * T + np.arange(T)
    pos_s = PAST + (np.arange(128) % LS)
    pos = np.concatenate([np.maximum(pos_pref, 0), pos_main, pos_s])
    ropeA = _rope_tab(pos, 32).reshape(NA, 128, 2, 32)
    ropeR = _rope_tab(pos, 128).reshape(NA, 128, 2, 128)
    k_ = np.arange(128)[:, None]
    q_ = np.arange(128)[None, :]
    maskO = np.where(q_ >= k_, 0.0, NEG).astype(np.float32)
    maskP = np.where(q_ < k_, 0.0, NEG).astype(np.float32)
    maskF = maskP if half_idx > 0 else np.full((128, 128), NEG, np.float32)
    ident = np.eye(128, dtype=np.float32)
    tb, tl = np.arange(128) // LS, np.arange(128) % LS
    smask = np.full((NBS, 128, 128), NEG, np.float32)
    for b in range(NBS):
        j = np.arange(128)[:, None]
        ok = (tb[None, :] == b) & (j > tl[None, :])
        smask[b] = np.where(ok, 0.0, NEG)
    snew = np.where((tb[:, None] == tb[None, :]) & (tl[:, None] <= tl[None, :]), 0.0, NEG).astype(np.float32)
    cst_bf = np.concatenate([ident] + [np.tile(m, (1, 2)) for m in (maskO, maskP, maskF, snew)], axis=1)
    cst_smask = np.tile(smask, (1, 1, 2))
    gam = 1.0 - 2.0 ** (-5.0 - np.arange(RH, dtype=np.float64))
    t = np.arange(128, dtype=np.float64)
    dec_p = np.concatenate([gam[None, :] ** (t[:, None] + 1), (gam[None, :] ** (-(t[:, None] + 1))) * DK ** -0.5], axis=1)
    l = (np.arange(128) % LS).astype(np.float64)
    dec_s = np.concatenate([gam[None, :] ** (l[:, None] + 1), (gam[None, :] ** (-(l[:, None] + 1))) * DK ** -0.5], axis=1)
    cm_p = (q_ >= k_).astype(np.float64)
    cm_s = ((tb[:, None] == tb[None, :]) & (tl[None, :] >= tl[:, None])).astype(np.float64)
    rm8 = np.zeros((128, NBS * 4))
    for b in range(NBS):
        rm8[:, 4 * b:4 * b + 4] = (tb[:, None] == b) * (gam[None, :] ** 8)
    cst_f = np.concatenate([dec_p, dec_s, cm_p, cm_s, rm8, np.zeros((128, 32))], axis=1).astype(np.float32)
    return dict(ropeA=ropeA, ropeR=ropeR, cst_bf=cst_bf.astype(np.float32), cst_smask=cst_smask.astype(np.float32),
                cst_f=cst_f)


def make_in_maps(inputs, NCH, n_cores):
    T = NCH * 128
    f = lambda a: np.ascontiguousarray(np.asarray(a, dtype=np.float32))
    xp, xs = f(inputs["x_prompt"]), f(inputs["x_sample"])
    pp, ps_ = f(inputs["p_prompt"]), f(inputs["p_sample"])
    shared = dict(
        w_in_attn=f(inputs["w_in_attn"])[0], w_out_attn=f(inputs["w_out_attn"])[0],
        w_in_ret=f(inputs["w_in_ret"])[0], w_out_ret=f(inputs["w_out_ret"])[0],
        w_ple=f(inputs["w_ple"]), w_gate=f(inputs["w_ple_gate"]),
        norms=np.concatenate([f(inputs["pre_norm"]), f(inputs["post_norm"])], axis=0),
        sinks=f(inputs["attn_sinks"]),
    )
    maps = []
    for c in range(n_cores):
        b, h = c // 2, c % 2
        own = slice(h * T, (h + 1) * T)
        prev = slice((h - 1) * T, h * T)
        sb_ = slice(c * NBS, (c + 1) * NBS)
        zx = np.zeros((T, D), np.float32)
        zp = np.zeros((T, PLE), np.float32)
        m = dict(shared)
        m["x_all"] = np.concatenate([xp[b, prev] if h else zx, xp[b, own], xs[sb_].reshape(128, D)], axis=0)
        m["p0_all"] = np.concatenate([pp[0, b, prev] if h else zp, pp[0, b, own], ps_[0, sb_].reshape(128, PLE)], axis=0)
        m["p1_all"] = np.concatenate([pp[1, b, own], ps_[1, sb_].reshape(128, PLE)], axis=0)
        m["cache_k"] = f(inputs["cache_k_win"])[0, sb_].reshape(NBS, 128, 256)
        m["cache_v"] = f(inputs["cache_v_win"])[0, sb_].reshape(NBS, 128, 256)
        m["state_in"] = f(inputs["state_ret"])[0, sb_]
        m.update(make_consts(NCH, h))
        cstf = m["cst_f"].copy()
        cstf[:, -32:] = shared["norms"].reshape(4, 8, 128).transpose(2, 0, 1).reshape(128, 32)
        m["cst_f"] = cstf
        maps.append(m)
    return maps


def assemble(results, NCH, n_cores):
    nb = n_cores // 2
    T = NCH * 128
    y_p = np.zeros((nb, 2 * T, D), np.float32)
    y_s = np.zeros((n_cores * NBS, LS, D), np.float32)
    kwp = np.zeros((1, nb, 128, NKV, HD), np.float32)
    vwp = np.zeros_like(kwp)
    kws = np.zeros((1, n_cores * NBS, 128, NKV, HD), np.float32)
    vws = np.zeros_like(kws)
    rsp = np.zeros((1, nb, RH, DK, DV), np.float32)
    rss = np.zeros((1, n_cores * NBS, RH, DK, DV), np.float32)
    for c in range(n_cores):
        r = results[c]
        b, h = c // 2, c % 2
        y_p[b, h * T:(h + 1) * T] = r["y_p"]
        y_s[c * NBS:(c + 1) * NBS] = np.asarray(r["y_s"]).reshape(NBS, LS, D)
        kws[0, c * NBS:(c + 1) * NBS] = np.asarray(r["kwin_s"]).reshape(NBS, 128, NKV, HD)
        vws[0, c * NBS:(c + 1) * NBS] = np.asarray(r["vwin_s"]).reshape(NBS, 128, NKV, HD)
        rss[0, c * NBS:(c + 1) * NBS] = r["rst_s"]
        if h == 1:
            kwp[0, b] = np.asarray(r["kwin_p"]).reshape(128, NKV, HD)
            vwp[0, b] = np.asarray(r["vwin_p"]).reshape(128, NKV, HD)
            rsp[0, b] = r["rst_p"]
    return (y_p, y_s, kwp, vwp, kws, vws, rsp, rss)


_NC_CACHE = {}


def kernel(**inputs):
    NCH, n_cores = 16, 8
    if NCH not in _NC_CACHE:
        _NC_CACHE[NCH] = build_program(NCH)
    nc = _NC_CACHE[NCH]
    in_maps = make_in_maps(inputs, NCH, n_cores)
    res = run_bass_kernel_spmd(nc, in_maps, core_ids=list(range(n_cores)))
    return assemble(res.results, NCH, n_cores)
```
